# Optimizing a Trainium2 kernel written in Bass

```python
import math
import jax, jax.numpy as jnp
from jax import lax
import numpy as np

D_MODEL = 1024
BATCH = 8
SEQ = 2048
DEPTH = 4

MEM_LEN = 256
CONV_W = 512
CONV_WIDTH = 3
ATT_HEADS = 8
ATT_DH = 64
ATT_W = ATT_HEADS * ATT_DH
MOBA_BLOCK = 256
MOBA_TOPK = 3
Q_CHUNK = 16
MEM_HEADS = 4
MEM_DH = 128
MEM_W = MEM_HEADS * MEM_DH
NUM_BUCKETS = 32
MAX_EXACT = NUM_BUCKETS // 2
MAX_DISTANCE = 1024
D_FF = 2816
FFN_CONV_WIDTH = 3
N_BRANCHES = 3
RMS_EPS = 1e-6
NEG = -1e30

IN_SPLITS = [CONV_W, CONV_W, CONV_W, ATT_W, ATT_W, ATT_W, MEM_W, D_MODEL, D_MODEL, D_MODEL]
IN_COLS = sum(IN_SPLITS)
IN_OFFSETS = list(np.cumsum(IN_SPLITS)[:-1])

kernel_name = "hybrid_conv_moba_mem_gated_block"


def rmsnorm(x, g):
    x32 = x.astype(jnp.float32)
    y = x32 * lax.rsqrt(jnp.mean(x32 * x32, axis=-1, keepdims=True) + RMS_EPS)
    return (y * g.astype(jnp.float32)).astype(x.dtype)


def causal_dwconv(u, w, b=None):
    width, c = w.shape
    y = lax.conv_general_dilated(
        u, w.astype(u.dtype)[:, None, :], window_strides=(1,), padding=[(width - 1, 0)],
        dimension_numbers=("NWC", "WIO", "NWC"), feature_group_count=c)
    if b is not None:
        y = y + b.astype(u.dtype)
    return y


def rel_bucket(dist):
    n = jnp.maximum(dist, 0)
    is_small = n < MAX_EXACT
    n_f = jnp.maximum(n, MAX_EXACT).astype(jnp.float32)
    large = MAX_EXACT + (jnp.log(n_f / MAX_EXACT) / math.log(MAX_DISTANCE / MAX_EXACT)
                         * (NUM_BUCKETS - MAX_EXACT)).astype(jnp.int32)
    large = jnp.minimum(large, NUM_BUCKETS - 1)
    return jnp.where(is_small, n, large)


def moba_attention(q, k, v, rel_bias):
    b_, h_, s_, dh = q.shape
    nb = -(-s_ // MOBA_BLOCK)
    pad = nb * MOBA_BLOCK - s_
    k_pad = jnp.pad(k, ((0, 0), (0, 0), (0, pad), (0, 0)))
    v_pad = jnp.pad(v, ((0, 0), (0, 0), (0, pad), (0, 0)))
    kb = k_pad.reshape(b_, h_, nb, MOBA_BLOCK, dh)
    vb = v_pad.reshape(b_, h_, nb, MOBA_BLOCK, dh)
    scale = dh ** -0.5

    k_mean = jnp.mean(kb.astype(jnp.float32), axis=3)
    gate = jnp.einsum("bhsd,bhnd->bhsn", q.astype(jnp.float32), k_mean)
    q_blk = jnp.arange(s_) // MOBA_BLOCK
    past = jnp.arange(nb)[None, :] < q_blk[:, None]
    gate = jnp.where(past, gate, NEG)
    k_sel = max(1, min(MOBA_TOPK, nb - 1))
    _, sel = lax.top_k(gate, k_sel)

    bias_hb = rel_bias.astype(jnp.float32).T
    b_idx = jnp.arange(b_)[:, None, None, None]
    h_idx = jnp.arange(h_)[None, :, None, None]
    blk_off = jnp.arange(MOBA_BLOCK)

    def chunk(c):
        start = c * Q_CHUNK
        qc = lax.dynamic_slice_in_dim(q, start, Q_CHUNK, axis=2)
        selc = lax.dynamic_slice_in_dim(sel, start, Q_CHUNK, axis=2)
        t = start + jnp.arange(Q_CHUNK)
        valid = jnp.arange(k_sel)[None, :] < (t // MOBA_BLOCK)[:, None]
        kg = kb[b_idx, h_idx, selc]
        vg = vb[b_idx, h_idx, selc]
        s_sel = jnp.einsum("bhqd,bhqjkd->bhqjk", qc, kg).astype(jnp.float32) * scale
        k_pos = selc[..., None] * MOBA_BLOCK + blk_off
        bucket = rel_bucket(t[None, None, :, None, None] - k_pos)
        bias = bias_hb[h_idx[..., None], bucket]
        s_sel = jnp.where(valid[None, None, :, :, None], s_sel + bias, NEG)
        own = start // MOBA_BLOCK
        ko = lax.dynamic_slice_in_dim(k_pad, own * MOBA_BLOCK, MOBA_BLOCK, axis=2)
        vo = lax.dynamic_slice_in_dim(v_pad, own * MOBA_BLOCK, MOBA_BLOCK, axis=2)
        s_own = jnp.einsum("bhqd,bhkd->bhqk", qc, ko).astype(jnp.float32) * scale
        dist = t[:, None] - (own * MOBA_BLOCK + blk_off)[None, :]
        bias_own = bias_hb[:, rel_bucket(dist)]
        s_own = jnp.where(dist[None, None] >= 0, s_own + bias_own[None], NEG)
        logits = jnp.concatenate([s_sel.reshape(b_, h_, Q_CHUNK, k_sel * MOBA_BLOCK), s_own], axis=-1)
        p = jax.nn.softmax(logits, axis=-1).astype(v.dtype)
        p_sel = p[..., : k_sel * MOBA_BLOCK].reshape(b_, h_, Q_CHUNK, k_sel, MOBA_BLOCK)
        p_own = p[..., k_sel * MOBA_BLOCK:]
        return (jnp.einsum("bhqjk,bhqjkd->bhqd", p_sel, vg)
                + jnp.einsum("bhqk,bhkd->bhqd", p_own, vo))

    out = lax.map(chunk, jnp.arange(s_ // Q_CHUNK))
    return jnp.transpose(out, (1, 2, 0, 3, 4)).reshape(b_, h_, s_, dh)


def memory_attention(qm, mem_n, w_mem_kv):
    b_, s_, _ = qm.shape
    kv = mem_n @ w_mem_kv
    mk, mv = jnp.split(kv, 2, axis=-1)
    mk = mk.reshape(b_, -1, MEM_HEADS, MEM_DH)
    mv = mv.reshape(b_, -1, MEM_HEADS, MEM_DH)
    q4 = qm.reshape(b_, s_, MEM_HEADS, MEM_DH)
    s = jnp.einsum("bshd,bmhd->bhsm", q4, mk).astype(jnp.float32) * (MEM_DH ** -0.5)
    p = jax.nn.softmax(s, axis=-1).astype(qm.dtype)
    return jnp.einsum("bhsm,bmhd->bshd", p, mv).reshape(b_, s_, MEM_W)


def setup_inputs(seed: int = 0) -> dict:
    key = jax.random.key(seed)
    ks = jax.random.split(key, 20)

    def nrm(k, shape, scale):
        return jax.random.normal(k, shape, jnp.float32) * scale

    def gain(k):
        return 1.0 + nrm(k, (DEPTH, D_MODEL), 0.05)

    return {
        "x": nrm(ks[0], (BATCH, SEQ, D_MODEL), 1.0),
        "mem": nrm(ks[1], (BATCH, MEM_LEN, D_MODEL), 1.0),
        "rel_bias": nrm(ks[2], (NUM_BUCKETS, ATT_HEADS), 0.5),
        "g_pre_mix": gain(ks[3]),
        "g_post_mix": gain(ks[4]),
        "g_pre_ffn": gain(ks[5]),
        "g_post_ffn": gain(ks[6]),
        "g_mem": gain(ks[7]),
        "w_in": nrm(ks[8], (DEPTH, D_MODEL, IN_COLS), D_MODEL ** -0.5),
        "b_gate": nrm(ks[9], (DEPTH, N_BRANCHES * D_MODEL), 0.01),
        "conv_mix_w": nrm(ks[10], (DEPTH, CONV_WIDTH, CONV_W), CONV_WIDTH ** -0.5),
        "w_conv_out": nrm(ks[11], (DEPTH, CONV_W, D_MODEL), CONV_W ** -0.5),
        "w_attn_out": nrm(ks[12], (DEPTH, ATT_W, D_MODEL), ATT_W ** -0.5),
        "w_mem_kv": nrm(ks[13], (DEPTH, D_MODEL, 2 * MEM_W), D_MODEL ** -0.5),
        "w_mem_out": nrm(ks[14], (DEPTH, MEM_W, D_MODEL), MEM_W ** -0.5),
        "w_out": nrm(ks[15], (DEPTH, D_MODEL, D_MODEL), D_MODEL ** -0.5),
        "w_up": nrm(ks[16], (DEPTH, D_MODEL, 2 * D_FF), D_MODEL ** -0.5),
        "ffn_conv_w": nrm(ks[17], (DEPTH, FFN_CONV_WIDTH, 2 * D_FF), FFN_CONV_WIDTH ** -0.5),
        "ffn_conv_b": nrm(ks[18], (DEPTH, 2 * D_FF), 0.01),
        "w_down": nrm(ks[19], (DEPTH, D_FF, D_MODEL), D_FF ** -0.5),
    }


def reference(x, mem, rel_bias, g_pre_mix, g_post_mix, g_pre_ffn, g_post_ffn, g_mem,
              w_in, b_gate, conv_mix_w, w_conv_out, w_attn_out, w_mem_kv, w_mem_out,
              w_out, w_up, ffn_conv_w, ffn_conv_b, w_down):
    b_, s_, d_ = x.shape
    for l in range(DEPTH):
        h = rmsnorm(x, g_pre_mix[l])
        proj = h @ w_in[l]
        (cb, cc, cv, q, k, v, qm, ga, gb, gc) = jnp.split(proj, IN_OFFSETS, axis=-1)
        y_a = (cb * causal_dwconv(cc * cv, conv_mix_w[l])) @ w_conv_out[l]
        to_heads = lambda t: jnp.transpose(t.reshape(b_, s_, ATT_HEADS, ATT_DH), (0, 2, 1, 3))
        o = moba_attention(to_heads(q), to_heads(k), to_heads(v), rel_bias)
        y_b = jnp.transpose(o, (0, 2, 1, 3)).reshape(b_, s_, ATT_W) @ w_attn_out[l]
        y_c = memory_attention(qm, rmsnorm(mem, g_mem[l]), w_mem_kv[l]) @ w_mem_out[l]
        bg = b_gate[l].reshape(N_BRANCHES, d_)
        merged = (jax.nn.sigmoid(ga + bg[0]) * y_a + jax.nn.sigmoid(gb + bg[1]) * y_b
                  + jax.nn.sigmoid(gc + bg[2]) * y_c)
        x = x + rmsnorm(merged @ w_out[l], g_post_mix[l])
        h = rmsnorm(x, g_pre_ffn[l])
        u = causal_dwconv(h @ w_up[l], ffn_conv_w[l], ffn_conv_b[l])
        a, g = jnp.split(u, 2, axis=-1)
        f = (jax.nn.gelu(a) * g) @ w_down[l]
        x = x + rmsnorm(f, g_post_ffn[l])
    return x
```

```python
import math
from contextlib import ExitStack
import numpy as np
import concourse.bass as bass
import concourse.mybir as mybir
from concourse.bass_utils import run_bass_kernel_spmd

F32 = mybir.dt.float32
BF16 = mybir.dt.bfloat16
ALU = mybir.AluOpType
AF = mybir.ActivationFunctionType
AX = mybir.AxisListType

D = 1024
S = 2048
NTT = 4
KC = 8
DFF = 2816
NCP = 22
DEPTH = 4
INC = 6656
OFF_CB, OFF_CC, OFF_CV, OFF_Q, OFF_K, OFF_V, OFF_QM, OFF_GA, OFF_GB, OFF_GC = (
    0, 512, 1024, 1536, 2048, 2560, 3072, 3584, 4608, 5632)
EBW = 1920
NPRM = 252
P_GPRE, P_GPOST, P_GFFN, P_GPFFN, P_GMEM, P_BG, P_CW, P_FW, P_FB = 0, 8, 16, 24, 32, 40, 64, 76, 208
NEGM = -30000.0
EPS = 1e-6
C_ONES, C_ID, C_E, C_AGG, C_END = 0, 128, 256, 1280, 1792
GELU_FUNC = [AF.Gelu_apprx_tanh]
USE_FAST_RECIP = [False]


def _esize(dtp):
    return 4 if dtp == F32 else 2


class Sched:
    ENGS = ("pe", "act", "dve", "pool", "sp")
    NDMA = 8
    GRAN = 64

    def __init__(self, nc, stack):
        self.nc = nc
        self.sem = {e: stack.enter_context(nc.semaphore("s_" + e)) for e in self.ENGS}
        self.dsem = {e: [stack.enter_context(nc.semaphore("d_%s%d" % (e, i)))
                         for i in range(self.NDMA)] for e in ("sp", "pool")}
        self.ndma = {e: 0 for e in ("sp", "pool")}
        self.ops = {e: [] for e in self.ENGS}
        self.seen = {e: {} for e in self.ENGS}
        self.gw = {}
        self.gr = {}
        self.track = {}

    def register(self, name, nbytes):
        n = (nbytes + self.GRAN - 1) // self.GRAN
        self.track[name] = nbytes
        self.gw[name] = [None] * n
        self.gr[name] = [None] * n

    def _range(self, ap):
        name = ap.name
        if name not in self.track:
            return None
        pat = ap.ap
        pstep = pat[0][0]
        es = _esize(ap.dtype)
        off = ap.offset % pstep if pstep else ap.offset
        ext = 1
        for st_, cn in pat[1:]:
            ext += (cn - 1) * abs(st_)
        lo = off * es
        hi = (off + ext) * es
        return name, lo // self.GRAN, (hi + self.GRAN - 1) // self.GRAN

    def _need(self, eng, ev, waits):
        if ev is None:
            return
        if ev[0] == "c":
            _, e, idx = ev
            if e == eng and e == "pe":
                return
            key = ("c", e)
            if self.seen[eng].get(key, -1) >= idx:
                return
            self.seen[eng][key] = idx
            self.ops[e][idx]["sig"] = True
            waits.append(ev)
        else:
            _, q, n = ev
            key = ("d", q, n % self.NDMA)
            if self.seen[eng].get(key, -1) >= n:
                return
            self.seen[eng][key] = n
            waits.append(ev)

    def op(self, eng, fn, r=(), w=(), dma=False):
        waits = []
        idx = len(self.ops[eng])
        rr = []
        ww = []
        for ap in r:
            g = self._range(ap)
            if g is None:
                continue
            (ww if g[0].startswith("ps") else rr).append(g)
        for ap in w:
            g = self._range(ap)
            if g is not None:
                ww.append(g)
        for name, a, b in rr:
            gw = self.gw[name]
            last = None
            for i in range(a, b):
                wv = gw[i]
                if wv is not last:
                    self._need(eng, wv, waits)
                    last = wv
        for name, a, b in ww:
            gw, gr = self.gw[name], self.gr[name]
            last = None
            for i in range(a, b):
                wv = gw[i]
                if wv is not last:
                    self._need(eng, wv, waits)
                    last = wv
                rd = gr[i]
                if rd:
                    for k, v in rd.items():
                        if k == "dma":
                            for dv in v:
                                self._need(eng, dv, waits)
                        elif k != eng:
                            self._need(eng, ("c", k, v), waits)
        rec = {"fn": fn, "waits": waits, "sig": False, "dma": None}
        if dma:
            n = self.ndma[eng]
            self.ndma[eng] += 1
            if n >= self.NDMA:
                self._need(eng, ("d", eng, n - self.NDMA), waits)
            rec["dma"] = n
            ev = ("d", eng, n)
        else:
            ev = ("c", eng, idx)
        self.ops[eng].append(rec)
        for name, a, b in rr:
            gr = self.gr[name]
            for i in range(a, b):
                rd = gr[i]
                if rd is None:
                    rd = gr[i] = {}
                if dma:
                    rd.setdefault("dma", []).append(ev)
                else:
                    rd[eng] = idx
        for name, a, b in ww:
            gw, gr = self.gw[name], self.gr[name]
            for i in range(a, b):
                gw[i] = ev
                gr[i] = None
        return ev

    def emit(self, block, final_events=()):
        fw = []
        for ev in final_events:
            self._need("sp", ev, fw)
        self.ops["sp"].append({"fn": None, "waits": fw, "sig": False, "dma": None})
        cnt = {}
        for e in self.ENGS:
            c = 0
            lst = []
            for rec in self.ops[e]:
                if rec["sig"]:
                    c += 1
                lst.append(c)
            cnt[e] = lst
        hsem, dsem, ND = self.sem, self.dsem, self.NDMA

        def run(e, h):
            for rec in self.ops[e]:
                for ev in rec["waits"]:
                    if ev[0] == "c":
                        h.wait_ge(hsem[ev[1]], cnt[ev[1]][ev[2]])
                    else:
                        h.wait_ge(dsem[ev[1]][ev[2] % ND], 16 * (ev[2] // ND + 1))
                if rec["fn"] is None:
                    continue
                ins = rec["fn"](h)
                if rec["dma"] is not None:
                    ins.then_inc(dsem[e][rec["dma"] % ND], 16)
                elif rec["sig"]:
                    ins.then_inc(hsem[e], 1)

        @block.tensor
        def _(h):
            run("pe", h)

        @block.scalar
        def _(h):
            run("act", h)

        @block.vector
        def _(h):
            run("dve", h)

        @block.gpsimd
        def _(h):
            run("pool", h)

        @block.sync
        def _(h):
            run("sp", h)


def build_program(NL=DEPTH, dbg=None, stop_after=None):
    nc = bass.Bass("TRN2", target_bir_lowering=False)
    dt = nc.dram_tensor
    xT_d = dt("xT", [D, S], F32, kind="ExternalInput").ap()
    memT_d = dt("memT", [D, 256], F32, kind="ExternalInput").ap()
    prm_d = dt("prm", [128, DEPTH * NPRM], F32, kind="ExternalInput").ap()
    tbl_d = dt("tbl", [8, 128, EBW], F32, kind="ExternalInput").ap()
    b31_d = dt("b31", [128, 8], F32, kind="ExternalInput").ap()
    cst_d = dt("cst", [128, C_END], F32, kind="ExternalInput").ap()
    w_in_d = dt("w_in", [DEPTH, D, INC], F32, kind="ExternalInput").ap()
    w_co_d = dt("w_conv_out", [DEPTH, 512, D], F32, kind="ExternalInput").ap()
    w_ao_d = dt("w_attn_out", [DEPTH, 512, D], F32, kind="ExternalInput").ap()
    w_kv_d = dt("w_mem_kv", [DEPTH, D, 1024], F32, kind="ExternalInput").ap()
    w_mo_d = dt("w_mem_out", [DEPTH, 512, D], F32, kind="ExternalInput").ap()
    w_out_d = dt("w_out", [DEPTH, D, D], F32, kind="ExternalInput").ap()
    w_up_d = dt("w_up", [DEPTH, D, 2 * DFF], F32, kind="ExternalInput").ap()
    w_dn_d = dt("w_down", [DEPTH, DFF, D], F32, kind="ExternalInput").ap()
    yT_d = dt("yT", [D, S], F32, kind="ExternalOutput").ap()
    ebx_d = dt("ebx", [8, 128, EBW], BF16, kind="Internal").ap()
    dbg_d = {}
    if dbg:
        for name, shape in dbg.items():
            dbg_d[name] = dt("dbg_" + name, list(shape), F32, kind="ExternalOutput").ap()

    with ExitStack() as st:
        sc = Sched(nc, st)

        def sb(name, shape, dtp):
            t = st.enter_context(nc.sbuf_tensor(name, shape, dtp))
            n = 1
            for s_ in shape[1:]:
                n *= s_
            sc.register(name, n * _esize(dtp))
            return t

        xT = sb("xTs", [128, KC, S], F32)
        hTt = sb("hTs", [128, KC * S], BF16)
        Rt = sb("Rs", [128, 32768], BF16)
        EBt = sb("EBs", [128, 2, EBW], BF16)
        wbt = sb("wbs", [128, 8448], BF16)
        ARt = sb("ARs", [128, 5632], BF16)
        prm = sb("prms", [128, DEPTH * NPRM], F32)
        cst = sb("csts", [128, C_END], BF16)
        b31 = sb("b31s", [128, 8], F32)
        epsb = sb("epsb", [128, 1], F32)
        ksum = sb("ksums", [128, 4, 8], F32)
        dK32 = sb("dK32s", [128, 4, 64], F32)
        dKh = sb("dKhs", [128, 4, 128], BF16)
        dKl = sb("dKls", [128, 4, 128], BF16)
        hal = sb("hals", [128, 2 * NCP, 2], BF16)
        ps = []
        for i in range(8):
            ps.append(st.enter_context(nc.psum_tensor("ps%d" % i, [128, 512], F32)))
            sc.register("ps%d" % i, 2048)
        block = st.enter_context(nc.Block())

        hT = hTt[:, :].rearrange("p (c t) -> p c t", c=KC)
        R = Rt[:, :]
        AR = ARt[:, :]
        ones = cst[:, C_ONES:C_ONES + 128]
        ident = cst[:, C_ID:C_ID + 128]
        final_events = []

        def v3(ap2, c):
            return ap2.rearrange("p (c t) -> p c t", c=c)

        def isap(x):
            return not isinstance(x, (int, float)) and x is not None

        def MM(out, lhsT, rhs, start, stop, skip=False):
            if skip:
                sc.op("pe", lambda h: h.matmul(out, lhsT, rhs, start=start, stop=stop, skip_group_check=True),
                      r=[lhsT, rhs], w=[out])
            else:
                sc.op("pe", lambda h: h.matmul(out, lhsT, rhs, start=start, stop=stop), r=[lhsT, rhs], w=[out])

        def ACT(out, in_, func, bias=None, scale=1.0):
            r = [in_] + ([bias] if isap(bias) else [])
            if bias is None:
                sc.op("act", lambda h: h.activation(out=out, in_=in_, func=func, scale=scale), r=r, w=[out])
            else:
                sc.op("act", lambda h: h.activation(out=out, in_=in_, func=func, bias=bias, scale=scale), r=r, w=[out])

        def TT(eng, out, in0, in1, op):
            sc.op(eng, lambda h: h.tensor_tensor(out=out, in0=in0, in1=in1, op=op), r=[in0, in1], w=[out])

        def TS(eng, out, in0, s1, s2, op0, op1=None):
            r = [in0] + [x for x in (s1, s2) if isap(x)]
            if op1 is None:
                sc.op(eng, lambda h: h.tensor_scalar(out=out, in0=in0, scalar1=s1, scalar2=None, op0=op0), r=r, w=[out])
            else:
                sc.op(eng, lambda h: h.tensor_scalar(out=out, in0=in0, scalar1=s1, scalar2=s2, op0=op0, op1=op1),
                      r=r, w=[out])

        def STT(out, in0, scalar, in1, op0, op1):
            r = [in0, in1] + ([scalar] if isap(scalar) else [])
            sc.op("dve", lambda h: h.scalar_tensor_tensor(out=out, in0=in0, scalar=scalar, in1=in1, op0=op0, op1=op1),
                  r=r, w=[out])

        def COPY(eng, out, in_):
            if eng == "act":
                ACT(out, in_, AF.Copy)
            else:
                sc.op(eng, lambda h: h.tensor_copy(out=out, in_=in_), r=[in_], w=[out])

        def RECIP(out, in_):
            sc.op("dve", lambda h: h.reciprocal(out=out, in_=in_), r=[in_], w=[out])

        def RECIPF(out, in_):
            if USE_FAST_RECIP[0]:
                sc.op("dve", lambda h: h.reciprocal_approx_fast(out=out, in_=in_), r=[in_], w=[out])
            else:
                RECIP(out, in_)

        def MEMSET(eng, out, val):
            sc.op(eng, lambda h: h.memset(out, val), w=[out])

        def DMA(q, out, in_):
            return sc.op(q, lambda h: h.dma_start(out=out, in_=in_), r=[in_], w=[out], dma=True)

        wstate = {"n": 0, "n3": 0}

        def wload(parts, three=False):
            if three:
                base = (wstate["n3"] % 3) * 2816
                wstate["n3"] += 1
            else:
                base = (wstate["n"] % 2) * 4096
                wstate["n"] += 1
            for (off, kc, ncols, src) in parts:
                DMA("pool", wbt[:, base + off:base + off + kc * ncols].rearrange("p (k n) -> p k n", k=kc), src)
            return base

        def wview(base, off, kc, ncols):
            return wbt[:, base + off:base + off + kc * ncols].rearrange("p (k n) -> p k n", k=kc)

        def wsrc(w_d, l, nk, c0, ncols):
            return w_d[l, 0:nk * 128, c0:c0 + ncols].rearrange("(k p) n -> p k n", p=128)

        steps = []

        preissued = []

        def run_steps(depth=1, next_ld=None):
            n = len(steps)
            slots = [None] * n
            issued = 0
            if preissued and n and steps[0][0] is not None:
                slots[0] = preissued.pop(0)
                issued = 1

            def issue_upto(k):
                nonlocal issued
                while issued < min(k + 1, n):
                    if steps[issued][0] is not None:
                        slots[issued] = steps[issued][0]()
                    issued += 1
            for i, (ld, comp) in enumerate(steps):
                issue_upto(i + depth)
                if i == n - 1 and next_ld is not None:
                    preissued.append(next_ld())
                comp(slots[i])
            steps.clear()

        rot = {"n": 0}

        def next_ps(lo=0, n=8):
            i = lo + rot["n"] % n
            rot["n"] += 1
            return i

        def pcol(l, off):
            return prm[:, l * NPRM + off:l * NPRM + off + 1]

        def ar_bf(off, n):
            return AR[:, off:off + n]

        def ar_f32(off, n):
            return AR[:, off:off + n].bitcast(F32)

        def tsl(tt):
            return slice(tt * 512, (tt + 1) * 512)

        sc.register("ebx", EBW * 2)
        for h_ in range(8):
            DMA("pool", EBt[:, h_ % 2, :], tbl_d[h_])
            ACT(EBt[:, h_ % 2, :], EBt[:, h_ % 2, :], AF.Exp)
            DMA("sp", ebx_d[h_], EBt[:, h_ % 2, :])
        DMA("sp", prm[:, :], prm_d)
        DMA("sp", b31[:, :], b31_d)
        DMA("pool", cst[:, :], cst_d)
        MEMSET("dve", epsb[:, :], EPS)
        for c in range(KC):
            DMA("sp", xT[:, c, :], xT_d[c * 128:(c + 1) * 128, :])

        def rstd_from_ps(psi, ncol, rs_ap, rstd_ap):
            ACT(rs_ap, ps[psi][:, 0:ncol], AF.Ln, bias=epsb[:, 0:1], scale=1.0 / D)
            ACT(rstd_ap, rs_ap, AF.Exp, scale=-0.5)

        nk_ = {"n": 0}
        EBf = EBt[:, :, :].rearrange("p a b -> p (a b)")
        tmpA = EBf[:, 0:1024].bitcast(F32)
        rstdR = [EBf[:, 1024:2048].bitcast(F32), EBf[:, 2048:3072].bitcast(F32)]
        NSQ = [ar_bf(3584, 512), ar_bf(4096, 512)]
        NRSTD = ar_f32(4608, 1024)

        def rstd_ps(bank, rstd):
            ACT(rstd, ps[bank][:, :], AF.Ln, bias=epsb[:, 0:1], scale=1.0 / D)
            ACT(rstd, rstd, AF.Exp, scale=-0.5)

        def norm_tt(l, goff, tt):
            psi = next_ps(0, 4)
            for c in range(KC):
                k = nk_["n"]
                nk_["n"] += 1
                ACT(NSQ[k % 2], xT[:, c, tsl(tt)], AF.Square)
                MM(ps[psi][:, :], ones, NSQ[k % 2], c == 0, c == KC - 1)
            rstd_ps(psi, NRSTD)
            for c in range(KC):
                STT(hT[:, c, tsl(tt)], xT[:, c, tsl(tt)], pcol(l, goff + c), NRSTD, ALU.mult, ALU.mult)

        def resid_apply(l, goff, tt, ov, rstd, tmps):
            for c in range(KC):
                k = nk_["n"]
                nk_["n"] += 1
                tmp = tmps[k % len(tmps)]
                STT(tmp, ov(c), pcol(l, goff + c), rstd, ALU.mult, ALU.mult)
                TT("pool", xT[:, c, tsl(tt)], xT[:, c, tsl(tt)], tmp, ALU.add)

        dq = []

        def make_pieces(l_res, goff_res, tt, ov, rstd, l_norm, goff_norm):
            P = []
            for c in range(KC):
                def rp(c=c):
                    STT(tmpA, ov(c), pcol(l_res, goff_res + c), rstd, ALU.mult, ALU.mult)
                    TT("pool", xT[:, c, tsl(tt)], xT[:, c, tsl(tt)], tmpA, ALU.add)
                P.append(rp)
            if l_norm is not None:
                def stats():
                    psi = next_ps(0, 4)
                    for c in range(KC):
                        k = nk_["n"]
                        nk_["n"] += 1
                        ACT(NSQ[k % 2], xT[:, c, tsl(tt)], AF.Square)
                        MM(ps[psi][:, :], ones, NSQ[k % 2], c == 0, c == KC - 1)
                    rstd_ps(psi, NRSTD)
                P.append(stats)
                for c in range(KC):
                    P.append(lambda c=c: STT(hT[:, c, tsl(tt)], xT[:, c, tsl(tt)], pcol(l_norm, goff_norm + c),
                                             NRSTD, ALU.mult, ALU.mult))
            return P

        def drain(n):
            for _ in range(min(n, len(dq))):
                dq.pop(0)()

        Vv = R[:, 0:8192].rearrange("p (i c) -> p i c", i=16)
        QT = v3(R[:, 8192:16384], 4)
        KT = v3(R[:, 16384:24576], 4)
        Vh = [R[:, 24576 + s * 2048:24576 + (s + 1) * 2048].rearrange("p (i c) -> p i c", i=16)
              for s in range(2)]
        QTm = [R[:, 28672 + s * 2048:28672 + (s + 1) * 2048] for s in range(2)]
        early = {}

        def ld_cols_l(l, c0):
            return lambda: wload([(0, KC, 512, wsrc(w_in_d, l, KC, c0, 512))])

        def proj_T(s, dstT, tts=(0, 1, 2, 3)):
            W = wview(s, 0, KC, 512)
            for j in range(4):
                for tt in tts:
                    psi = next_ps()
                    for kc in range(KC):
                        MM(ps[psi][:, :], W[:, kc, j * 128:(j + 1) * 128], hT[:, kc, tsl(tt)],
                           kc == 0, kc == KC - 1)
                    COPY("act", dstT[:, j, tsl(tt)], ps[psi][:, :])

        def k_stats():
            for j in range(4):
                sc.op("dve", lambda h, j=j: h.tensor_reduce(
                    out=ksum[:, j, :], in_=KT[:, j, :].rearrange("p (n k) -> p n k", n=8),
                    axis=AX.X, op=ALU.add), r=[KT[:, j, :]], w=[ksum[:, j, :]])
                TT("dve", dK32[:, j, :].rearrange("p (n m) -> p n m", n=8),
                   ksum[:, j, :].unsqueeze(1).broadcast_to([128, 8, 8]),
                   ksum[:, j, :].unsqueeze(2).broadcast_to([128, 8, 8]), ALU.subtract)
                COPY("dve", dKh[:, j, 0:64], dK32[:, j, :])
                TT("dve", dKl[:, j, 0:64], dK32[:, j, :], dKh[:, j, 0:64], ALU.subtract)
                COPY("pool", dKh[:, j, 64:128], dKh[:, j, 0:64])
                COPY("pool", dKl[:, j, 64:128], dKl[:, j, 0:64])

        def v_proj(s, irange):
            W = wview(s, 0, KC, 512)
            for i in irange:
                psi = next_ps()
                for kc in range(KC):
                    MM(ps[psi][:, :], hT[:, kc, i * 128:(i + 1) * 128], W[:, kc, :], kc == 0, kc == KC - 1)
                COPY("act", Vv[:, i, :], ps[psi][:, :])

        def early_proj_steps(l2):
            def ka(s):
                early["K"] = s
                proj_T(s, KT, (0, 1))

            def va(s):
                early["V"] = s
                v_proj(s, range(8))
            return [(ld_cols_l(l2, OFF_K), ka), (ld_cols_l(l2, OFF_V), va)]

        def dump(name, view_fn, nchunk, ncol):
            tmpo = ar_f32(4096, 1024)
            for c in range(nchunk):
                for t0 in range(0, ncol, 512):
                    n = min(512, ncol - t0)
                    COPY("dve", tmpo[:, 0:n], view_fn(c, t0, n))
                    final_events.append(DMA("sp", dbg_d[name][c * 128:(c + 1) * 128, t0:t0 + n], tmpo[:, 0:n]))

        for l in range(NL):
            last = (l == NL - 1)
            if l == 0:
                for tt in range(NTT):
                    norm_tt(0, P_GPRE, tt)
            if dbg and "hT" in dbg and last:
                dump("hT", lambda c, t0, n: hT[:, c, t0:t0 + n], KC, S)

            def ld_cols(c0):
                return ld_cols_l(l, c0)

            def gate_ld(gate_off, w_bo_d, q4):
                return lambda: wload([(0, KC, 256, wsrc(w_in_d, l, KC, gate_off + q4 * 256, 256)),
                                      (2048, 4, 256, wsrc(w_bo_d, l, 4, q4 * 256, 256))])

            def ld_wo(g2):
                return lambda: wload([(0, KC, 512, wsrc(w_out_d, l, KC, g2 * 512, 512))])

            def ld_up(cp0):
                return lambda: wload([(0, KC, 256, wsrc(w_up_d, l, KC, cp0 * 128, 256)),
                                      (2048, KC, 256, wsrc(w_up_d, l, KC, DFF + cp0 * 128, 256))])

            def head_prep_tv(h_):
                e = h_ % 2
                hp = (h_ % 2) * 64
                DMA("sp", EBt[:, e, :], ebx_d[h_])
                voff, ooff = (0, 64) if e == 0 else (64, 0)
                zp = 64 - hp
                if h_ < 2:
                    MEMSET("pool", Vh[e][:, :, ooff:ooff + 64], 1.0)
                    MEMSET("pool", QTm[e][zp:zp + 64, :], 0.0)
                COPY("dve", Vh[e][:, :, voff:voff + 64], Vv[:, :, h_ * 64:(h_ + 1) * 64])

            def early_att_prep(_s):
                head_prep_tv(0)
                head_prep_tv(1)

            if l == 0:
                def comp_K(s):
                    proj_T(s, KT)
                    k_stats()
                steps.append((ld_cols(OFF_K), comp_K))
                steps.append((ld_cols(OFF_V), lambda s: v_proj(s, range(16))))
            else:
                def comp_Kb(_s):
                    proj_T(early["K"], KT, (2, 3))
                    k_stats()
                steps.append((None, comp_Kb))
                steps.append((None, lambda _s: v_proj(early["V"], range(8, 16))))
            steps.append((None, early_att_prep))
            steps.append((ld_cols(OFF_Q), lambda s: proj_T(s, QT)))
            run_steps(next_ld=None if stop_after in ("proj", "att") else gate_ld(OFF_GB, w_ao_d, 0))
            if dbg and "QT" in dbg and last:
                dump("QT", lambda c, t0, n: QT[:, c, t0:t0 + n], 4, S)
            if dbg and "KT" in dbg and last:
                dump("KT", lambda c, t0, n: KT[:, c, t0:t0 + n], 4, S)
            if stop_after == "proj":
                break

            LA, NSB, NPB = 4, 5, 6
            Pt = [ar_bf(i * 512, 512) for i in range(NPB)]
            ind = ar_bf(3072, 512)
            negs = [ar_bf(3584, 512), ar_bf(4096, 512)]
            rec = ar_f32(4608, 1024)

            def head_prep_q(h_):
                e = h_ % 2
                jc, hp = h_ // 2, (h_ % 2) * 64
                COPY("dve", QTm[e][hp:hp + 64, :], QT[hp:hp + 64, jc, :])

            def head_prep(h_):
                head_prep_tv(h_)
                head_prep_q(h_)

            def sel_prep1(h_, j):
                e, jc = h_ % 2, h_ // 2
                qsl = QTm[e][:, j * 512:(j + 1) * 512]
                MM(ps[0][:, :], dKh[:, jc, :], qsl, True, False)
                MM(ps[0][:, :], dKl[:, jc, :], qsl, False, True)
                TS("dve", ind, ps[0][:, :], 0.0, None, ALU.is_gt)

            def sel_prep2(h_, j):
                ng = negs[j % 2]
                for hf in range(2):
                    qb = 2 * j + hf
                    MM(ps[0][:, hf * 256:(hf + 1) * 256],
                       cst[:, C_AGG + (qb - 4) * 128:C_AGG + (qb - 3) * 128],
                       ind[:, hf * 256:(hf + 1) * 256], True, True, skip=True)
                TS("dve", ng, ps[0][:, :], 2.5, NEGM, ALU.is_ge, ALU.mult)

            tiles = [(h_, j, kt) for h_ in range(8) for j in range(NTT) for kt in range(4 * j + 4)]
            NT_ = len(tiles)

            def stageA(t):
                h_, j, kt = tiles[t]
                e, jc = h_ % 2, h_ // 2
                if (j, kt) == (0, 0):
                    sel_prep1(h_, 2)
                elif (j, kt) == (1, 2):
                    sel_prep2(h_, 2)
                elif (j, kt) == (1, 6):
                    sel_prep1(h_, 3)
                elif (j, kt) == (2, 4):
                    sel_prep2(h_, 3)
                q0, k0, n = j * 512, kt * 128, kt // 2
                qsl = QTm[e][:, q0:q0 + 512]
                pS = 1 + t % NSB
                use_mask = j >= 2 and n <= 2 * j
                MM(ps[pS][:, :], KT[:, jc, k0:k0 + 128], qsl, True, not use_mask)
                if use_mask:
                    MM(ps[pS][:, :], cst[:, C_E + n * 128:C_E + (n + 1) * 128], negs[j % 2], False, True)
                pt = Pt[t % NPB]
                delta = q0 - k0
                if delta >= 1024:
                    ACT(pt, ps[pS][:, :], AF.Exp, bias=b31[:, h_:h_ + 1], scale=0.125)
                else:
                    ACT(pt, ps[pS][:, :], AF.Exp, scale=0.125)
                    u0 = delta + 512
                    TT("dve" if t % 2 == 0 else "pool", pt, pt, EBt[:, e, u0:u0 + 512], ALU.mult)

            def stageC(t):
                h_, j, kt = tiles[t]
                e, jc = h_ % 2, h_ // 2
                nk = 4 * j + 4
                pO = 6 + (h_ * NTT + j) % 2
                MM(ps[pO][:, :], Vh[e][:, kt, :], Pt[t % NPB], kt == 0, kt == nk - 1)
                if kt == nk - 1:
                    vr = slice(0, 64) if e == 0 else slice(64, 128)
                    sr = slice(64, 128) if e == 0 else slice(0, 64)
                    ACT(rec[vr, :], ps[pO][sr, :], AF.Ln)
                    ACT(rec[vr, :], rec[vr, :], AF.Exp, scale=-1.0)
                    TT("dve", QT[vr, jc, j * 512:(j + 1) * 512], ps[pO][vr, :], rec[vr, :], ALU.mult)
                    if j == NTT - 1 and h_ + 2 < 8:
                        head_prep(h_ + 2)

            head_prep_q(0)
            head_prep_q(1)
            for t in range(NT_ + LA):
                if t < NT_:
                    stageA(t)
                if t >= LA:
                    stageC(t - LA)
            OT = QT
            if dbg and "OT" in dbg and last:
                dump("OT", lambda c, t0, n: OT[:, c, t0:t0 + n], 4, S)
            if stop_after == "att":
                break

            mrg = v3(R[:, 16384:32768], KC)
            sg = [ar_bf(0, 512), ar_bf(512, 512)]
            gtmp = [ar_bf(1024, 512), ar_bf(1536, 512)]
            gk = {"n": 0}

            def gate_branch(bidx, gate_off, w_bo_d, srcT, first):
                for q4 in range(4):
                    ld = gate_ld(gate_off, w_bo_d, q4)

                    def comp(s, q4=q4):
                        Wg = wview(s, 0, KC, 256)
                        Wb = wview(s, 2048, 4, 256)
                        for o2 in range(2):
                            oc = q4 * 2 + o2
                            for tt in range(NTT):
                                pg = next_ps()
                                for kc in range(KC):
                                    MM(ps[pg][:, :], Wg[:, kc, o2 * 128:(o2 + 1) * 128], hT[:, kc, tsl(tt)],
                                       kc == 0, kc == KC - 1)
                                k = gk["n"]
                                gk["n"] += 1
                                ACT(sg[k % 2], ps[pg][:, :], AF.Sigmoid, bias=pcol(l, P_BG + bidx * 8 + oc))
                                py = next_ps()
                                for kc in range(4):
                                    MM(ps[py][:, :], Wb[:, kc, o2 * 128:(o2 + 1) * 128], srcT[:, kc, tsl(tt)],
                                       kc == 0, kc == 3)
                                dst = mrg[:, oc, tsl(tt)]
                                if first:
                                    TT("dve", dst, ps[py][:, :], sg[k % 2], ALU.mult)
                                else:
                                    TT("dve", gtmp[k % 2], ps[py][:, :], sg[k % 2], ALU.mult)
                                    TT("pool", dst, dst, gtmp[k % 2], ALU.add)
                    steps.append((ld, comp))

            gate_branch(1, OFF_GB, w_ao_d, OT, True)
            run_steps(next_ld=None if stop_after == "brB" else ld_cols(OFF_CC))
            if stop_after == "brB":
                if dbg and "mrg" in dbg and last:
                    dump("mrg", lambda c, t0, n: mrg[:, c, t0:t0 + n], KC, S)
                break

            ccT = v3(R[:, 0:8192], 4)
            zT = v3(R[:, 8192:16384], 4)
            dgm = ar_bf(2048, 1536).rearrange("p (a b) -> p a b", a=12)
            for j in range(4):
                for tap in range(3):
                    TS("dve", dgm[:, j * 3 + tap, :], ident, pcol(l, P_CW + tap * 4 + j), None, ALU.mult)

            def comp_cc(s):
                proj_T(s, ccT)

            def comp_cv(s):
                W = wview(s, 0, KC, 512)
                for j in range(4):
                    for tt in range(NTT):
                        psi = next_ps()
                        for kc in range(KC):
                            MM(ps[psi][:, :], W[:, kc, j * 128:(j + 1) * 128], hT[:, kc, tsl(tt)],
                               kc == 0, kc == KC - 1)
                        TT("dve", zT[:, j, tsl(tt)], ps[psi][:, :], ccT[:, j, tsl(tt)], ALU.mult)

            def comp_cb(s):
                W = wview(s, 0, KC, 512)
                for j in range(4):
                    for tt in range(NTT):
                        pc = next_ps()
                        t0 = tt * 512
                        for tap in range(3):
                            sh = 2 - tap
                            lhs = dgm[:, j * 3 + tap, :]
                            if sh == 0:
                                MM(ps[pc][:, :], lhs, zT[:, j, t0:t0 + 512], tap == 0, True, skip=True)
                            elif tt == 0:
                                MM(ps[pc][:, sh:512], lhs, zT[:, j, 0:512 - sh], tap == 0, False, skip=True)
                            else:
                                MM(ps[pc][:, :], lhs, zT[:, j, t0 - sh:t0 - sh + 512], tap == 0, False, skip=True)
                        COPY("act", ccT[:, j, tsl(tt)], ps[pc][:, :])
                        psi = next_ps()
                        for kc in range(KC):
                            MM(ps[psi][:, :], W[:, kc, j * 128:(j + 1) * 128], hT[:, kc, tsl(tt)],
                               kc == 0, kc == KC - 1)
                        TT("dve", ccT[:, j, tsl(tt)], ps[psi][:, :], ccT[:, j, tsl(tt)], ALU.mult)

            qmT = v3(R[:, 0:8192], 4)
            Y = R[:, 8192:16384]
            memf = Y[:, 0:4096].bitcast(F32).rearrange("p (c m) -> p c m", c=KC)
            memn = Y[:, 4096:6144].rearrange("p (c m) -> p c m", c=KC)
            mkT = Y[:, 6144:7168].rearrange("p (c m) -> p c m", c=4)
            mv = Y[:, 7168:8192].rearrange("p (i c) -> p i c", i=2)
            msq8 = [ar_bf(3584 + 256 * c, 256) for c in range(KC)]
            mrs = EBf[:, 3072:3584].bitcast(F32)
            mrstd = EBf[:, 0:512].bitcast(F32)

            def mem_prep(_s):
                DMA("sp", memf, memT_d.rearrange("(c p) m -> p c m", p=128))
                for c in range(KC):
                    ACT(msq8[c], memf[:, c, :], AF.Square)

            def mem_prep2(_s):
                pm = next_ps()
                for c in range(KC):
                    MM(ps[pm][:, 0:256], ones, msq8[c], c == 0, c == KC - 1)
                rstd_from_ps(pm, 256, mrs, mrstd)
                for c in range(KC):
                    STT(memn[:, c, :], memf[:, c, :], pcol(l, P_GMEM + c), mrstd, ALU.mult, ALU.mult)

            steps.append((ld_cols(OFF_CC), comp_cc))
            steps.append((ld_cols(OFF_CV), comp_cv))
            steps.append((ld_cols(OFF_CB), comp_cb))
            steps.append((None, mem_prep))
            gate_branch(0, OFF_GA, w_co_d, ccT, False)
            steps.insert(len(steps) - 3, (None, mem_prep2))
            run_steps(next_ld=None if stop_after == "brA" else ld_cols(OFF_QM))
            if stop_after == "brA":
                if dbg and "mrg" in dbg and last:
                    dump("mrg", lambda c, t0, n: mrg[:, c, t0:t0 + n], KC, S)
                break

            def ld_kv(c0):
                return lambda: wload([(0, KC, 512, wsrc(w_kv_d, l, KC, c0, 512))])

            def comp_mk(s):
                W = wview(s, 0, KC, 512)
                for j in range(4):
                    psi = next_ps()
                    for kc in range(KC):
                        MM(ps[psi][:, 0:256], W[:, kc, j * 128:(j + 1) * 128], memn[:, kc, :],
                           kc == 0, kc == KC - 1)
                    COPY("act", mkT[:, j, :], ps[psi][:, 0:256])

            def comp_mv(s):
                W = wview(s, 0, KC, 512)
                for i in range(2):
                    psi = next_ps()
                    for kc in range(KC):
                        MM(ps[psi][:, :], memn[:, kc, i * 128:(i + 1) * 128], W[:, kc, :], kc == 0, kc == KC - 1)
                    COPY("act", mv[:, i, :], ps[psi][:, :])

            def comp_qm(s):
                proj_T(s, qmT)

            def mem_attn():
                NMP = 4
                mP = [ar_bf(i * 512, 512) for i in range(NMP)]
                mrec = ar_f32(3072, 1024)
                mt = [(hm, tt, i) for hm in range(4) for tt in range(NTT) for i in range(2)]

                def mA(t):
                    hm, tt, i = mt[t]
                    pS = 4 + t % 4
                    MM(ps[pS][:, :], mkT[:, hm, i * 128:(i + 1) * 128], qmT[:, hm, tsl(tt)], True, True)
                    ACT(mP[t % NMP], ps[pS][:, :], AF.Exp, scale=128.0 ** -0.5)

                def mC(t):
                    hm, tt, i = mt[t]
                    g = t // 2
                    pC, pSm = (g % 2) * 2, (g % 2) * 2 + 1
                    MM(ps[pC][:, :], mv[:, i, hm * 128:(hm + 1) * 128], mP[t % NMP], i == 0, i == 1)
                    MM(ps[pSm][:, :], ones, mP[t % NMP], i == 0, i == 1)
                    if i == 1:
                        ACT(mrec, ps[pSm][:, :], AF.Ln)
                        ACT(mrec, mrec, AF.Exp, scale=-1.0)
                        TT("dve", qmT[:, hm, tsl(tt)], ps[pC][:, :], mrec, ALU.mult)

                for t in range(len(mt) + 2):
                    if t < len(mt):
                        mA(t)
                    if t >= 2:
                        mC(t - 2)

            def comp_mv_att(s):
                comp_mv(s)
                mem_attn()

            steps.append((ld_cols(OFF_QM), comp_qm))
            steps.append((ld_kv(0), comp_mk))
            steps.append((ld_kv(512), comp_mv_att))
            gate_branch(2, OFF_GC, w_mo_d, qmT, False)
            run_steps(next_ld=None if stop_after == "brC" else ld_wo(0))
            if dbg and "mrg" in dbg and last:
                dump("mrg", lambda c, t0, n: mrg[:, c, t0:t0 + n], KC, S)
            if stop_after == "brC":
                break

            oT4 = R[:, 0:16384].rearrange("p (t c n) -> p t c n", t=4, c=KC)
            fdg = [ar_bf(s_ * 768, 768).rearrange("p (a b) -> p a b", a=6) for s_ in range(2)]
            gel = [ar_bf(1536, 512), ar_bf(2048, 512)]
            osq = [ar_bf(2560, 512), ar_bf(3072, 512)]
            rstdI = [ar_f32(1536, 1024), ar_f32(2560, 1024)]
            tmpB = ar_f32(0, 1024)
            ok = {"n": 0}
            sspend = []

            wo_slots = {}

            def comp_wo_all(s1):
                Ws = (wview(wo_slots["a"], 0, KC, 512), wview(s1, 0, KC, 512))
                rdst = [rstdI[0], rstdI[1], rstdR[0], rstdR[1]]

                def resid01(tt):
                    resid_apply(l, P_GPOST, tt, lambda c, tt=tt: oT4[:, tt, c, :], rstdI[tt], [tmpA, tmpB])
                for tt in range(NTT):
                    for oc in range(KC):
                        if tt == NTT - 1 and oc == 4 and stop_after != "mix":
                            preissued.append(ld_up(0)())
                        psi = next_ps(0, 4)
                        W = Ws[oc // 4]
                        o4 = oc % 4
                        for kc in range(KC):
                            MM(ps[psi][:, :], W[:, kc, o4 * 128:(o4 + 1) * 128], mrg[:, kc, tsl(tt)],
                               kc == 0, kc == KC - 1)
                        if sspend:
                            sspend.pop()()
                        if oc == 1 and tt >= 1:
                            rstd_ps(4 + tt - 1, rdst[tt - 1])
                            if tt == 1:
                                resid01(0)
                            elif tt == 2:
                                norm_tt(l, P_GFFN, 0)
                                resid01(1)
                            elif tt == 3:
                                norm_tt(l, P_GFFN, 1)
                        sq = osq[ok["n"] % 2]
                        ok["n"] += 1
                        COPY("act", oT4[:, tt, oc, :], ps[psi][:, :])
                        ACT(sq, ps[psi][:, :], AF.Square)
                        sspend.append(lambda sq=sq, tt=tt, oc=oc: MM(ps[4 + tt][:, :], ones, sq, oc == 0, oc == KC - 1))
                sspend.pop()()
                rstd_ps(7, rdst[3])

            steps.append((ld_wo(0), lambda s0: wo_slots.__setitem__("a", s0)))
            steps.append((ld_wo(1), comp_wo_all))
            run_steps()

            def deferred_b1(_s):
                for tt in (2, 3):
                    resid_apply(l, P_GPOST, tt, lambda c, tt=tt: oT4[:, tt, c, :], rstdR[tt - 2], [tmpA])
                    norm_tt(l, P_GFFN, tt)

            def deferred_b2(_s):
                for tt in (0, 1):
                    resid_apply(l, P_GPFFN, tt, lambda c, tt=tt: hT[:, c, tsl(tt)], rstdR[tt], [tmpA])
                    if l + 1 < NL:
                        norm_tt(l + 1, P_GPRE, tt)
            if stop_after == "mix":
                deferred_b1(None)
                break

            for hf in range(2):
                actT = R[:, 0:NCP * 1024].rearrange("p (c t) -> p c t", c=NCP)
                pre = R[:, 22528:22528 + 4104].rearrange("p (a b t) -> p a b t", a=2, b=2)
                fk = {"n": 0}

                pend = []

                def conv_stage(cp, k, sl):
                    def f():
                        for i in range(2):
                            pa, pg = next_ps(4, 4), next_ps(4, 4)
                            for ag, pp in ((0, pa), (1, pg)):
                                for tap in range(3):
                                    MM(ps[pp][:, :], fdg[sl][:, ag * 3 + tap, :],
                                       pre[:, sl, ag, i * 512 + tap:i * 512 + tap + 512], tap == 0, tap == 2)
                            ga = gel[(2 * k + i) % 2]
                            ACT(ga, ps[pa][:, :], GELU_FUNC[0], bias=pcol(l, P_FB + cp))
                            STT(actT[:, cp, tsl(i)], ps[pg][:, :], pcol(l, P_FB + NCP + cp), ga, ALU.add, ALU.mult)
                    return f

                def comp_up(cp0):
                    def f(s):
                        Ws = (wview(s, 0, KC, 256), wview(s, 2048, KC, 256))
                        for c2 in range(2):
                            cp = cp0 + c2
                            k = fk["n"]
                            fk["n"] += 1
                            sl = k % 2
                            for ag in range(2):
                                for tap in range(3):
                                    col = P_FW + tap * 44 + ag * NCP + cp
                                    TS("dve", fdg[sl][:, ag * 3 + tap, :], ident, pcol(l, col), None, ALU.mult)
                            for ag in range(2):
                                if hf == 0:
                                    MEMSET("dve", pre[:, sl, ag, 0:2], 0.0)
                                else:
                                    COPY("dve", pre[:, sl, ag, 0:2], hal[:, ag * NCP + cp, :])
                            for ag in range(2):
                                for i in range(2):
                                    psi = next_ps(0, 4)
                                    for kc in range(KC):
                                        MM(ps[psi][:, :], Ws[ag][:, kc, c2 * 128:(c2 + 1) * 128], hT[:, kc, tsl(2 * hf + i)],
                                           kc == 0, kc == KC - 1)
                                    COPY("act" if ag == 0 else "dve",
                                         pre[:, sl, ag, 2 + i * 512:2 + (i + 1) * 512], ps[psi][:, :])
                            if hf == 0:
                                for ag in range(2):
                                    COPY("dve", hal[:, ag * NCP + cp, :], pre[:, sl, ag, 1024:1026])
                            drain(5 if hf == 0 else 2)
                            if pend:
                                pend.pop()()
                            pend.append(conv_stage(cp, k, sl))
                    return f

                def fill_b1(_s):
                    for tt in (2, 3):
                        dq.extend(make_pieces(l, P_GPOST, tt, lambda c, tt=tt: oT4[:, tt, c, :], rstdR[tt - 2],
                                              l, P_GFFN))

                if hf == 0:
                    fill_b1(None)
                else:
                    for tt in (0, 1):
                        dq.extend(make_pieces(l, P_GPFFN, tt, lambda c, tt=tt: hT[:, c, tsl(tt)], rstdR[tt],
                                              (l + 1) if l + 1 < NL else None, P_GPRE))
                for cp0 in range(0, NCP, 2):
                    if hf == 0 and cp0 == 8:
                        steps.append((None, lambda _s: drain(len(dq))))
                    steps.append((ld_up(cp0), comp_up(cp0)))
                run_steps()
                pend.pop()()
                drain(len(dq))

                def ld_dn(oc):
                    return lambda: wload([(0, NCP, 128, wsrc(w_dn_d, l, NCP, oc * 128, 128))], three=True)

                def comp_dn(oc):
                    def f(s):
                        W = wview(s, 0, NCP, 128)
                        for i in range(2):
                            psi = next_ps(0, 4)
                            for cp in range(NCP):
                                MM(ps[psi][:, :], W[:, cp, :], actT[:, cp, tsl(i)], cp == 0, cp == NCP - 1)
                            if sspend:
                                sspend.pop()()
                            sq = osq[ok["n"] % 2]
                            ok["n"] += 1
                            COPY("act", hT[:, oc, tsl(2 * hf + i)], ps[psi][:, :])
                            ACT(sq, ps[psi][:, :], AF.Square)
                            sspend.append(lambda sq=sq, i=i, oc=oc: MM(ps[4 + i][:, :], ones, sq, oc == 0, oc == KC - 1))
                    return f

                for oc in range(KC):
                    steps.append((ld_dn(oc), comp_dn(oc)))

                def rstd_step(_s):
                    sspend.pop()()
                    rstd_ps(4, rstdR[0])
                    rstd_ps(5, rstdR[1])
                steps.append((None, rstd_step))
                run_steps(depth=2)
                if hf == 1 and l + 1 < NL:
                    steps.extend(early_proj_steps(l + 1))
                    run_steps()
                if hf == 1:
                    for tt in (2, 3):
                        resid_apply(l, P_GPFFN, tt, lambda c, tt=tt: hT[:, c, tsl(tt)], rstdR[tt - 2], [tmpA, tmpB])
                        if l + 1 < NL:
                            norm_tt(l + 1, P_GPRE, tt)

        for c in range(KC):
            final_events.append(DMA("sp", yT_d[c * 128:(c + 1) * 128, :], xT[:, c, :]))
        sc.emit(block, final_events)
    return nc


def _rel_bucket_np(dist):
    n = np.maximum(dist, 0)
    is_small = n < 16
    n_f = np.maximum(n, 16).astype(np.float32)
    v = (np.log(n_f / np.float32(16)) / np.float32(math.log(1024 / 16)) * np.float32(16))
    large = 16 + v.astype(np.int32)
    large = np.minimum(large, 31)
    return np.where(is_small, n, large)


def _host_prep(inp):
    f = np.float32
    prm = np.zeros((128, DEPTH * NPRM), f)
    for l in range(DEPTH):
        o = l * NPRM
        for name, off in (("g_pre_mix", P_GPRE), ("g_post_mix", P_GPOST), ("g_pre_ffn", P_GFFN),
                          ("g_post_ffn", P_GPFFN), ("g_mem", P_GMEM)):
            prm[:, o + off:o + off + 8] = np.asarray(inp[name][l], f).reshape(8, 128).T
        prm[:, o + P_BG:o + P_BG + 24] = np.asarray(inp["b_gate"][l], f).reshape(24, 128).T
        prm[:, o + P_CW:o + P_CW + 12] = np.asarray(inp["conv_mix_w"][l], f).reshape(12, 128).T
        prm[:, o + P_FW:o + P_FW + 132] = np.asarray(inp["ffn_conv_w"][l], f).reshape(132, 128).T
        prm[:, o + P_FB:o + P_FB + 44] = np.asarray(inp["ffn_conv_b"][l], f).reshape(44, 128).T
    rb = np.asarray(inp["rel_bias"], f)
    p = np.arange(128)[:, None]
    u = np.arange(EBW)[None, :]
    dist = u - 512 - p
    bucket = _rel_bucket_np(dist)
    tbl = np.empty((8, 128, EBW), f)
    for h in range(8):
        t = rb[bucket, h]
        tbl[h] = np.where(dist >= 0, t, f(NEGM))
    b31 = np.ascontiguousarray(np.broadcast_to(rb[31][None, :], (128, 8))).astype(f)
    cst = np.zeros((128, C_END), f)
    cst[:, C_ONES:C_ONES + 128] = 1.0
    cst[:, C_ID:C_ID + 128] = np.eye(128, dtype=f)
    for n in range(8):
        cst[n, C_E + n * 128:C_E + (n + 1) * 128] = 1.0
    for qb in range(4, 8):
        for n in range(qb):
            for m in range(qb):
                cst[n * 8 + m, C_AGG + (qb - 4) * 128 + n] = 1.0
    return prm, tbl, b31, cst


_W_NAMES = ("w_in", "w_conv_out", "w_attn_out", "w_mem_kv", "w_mem_out", "w_out", "w_up", "w_down")
_NC_CACHE = {}


def _in_maps(inputs, cores):
    prm, tbl, b31, cst = _host_prep(inputs)
    x = np.asarray(inputs["x"], np.float32)
    mem = np.asarray(inputs["mem"], np.float32)
    ws = {n: np.ascontiguousarray(np.asarray(inputs[n], np.float32)) for n in _W_NAMES}
    maps = []
    for b in cores:
        m = {"xT": np.ascontiguousarray(x[b].T), "memT": np.ascontiguousarray(mem[b].T),
             "prm": prm, "tbl": tbl, "b31": b31, "cst": cst}
        m.update(ws)
        maps.append(m)
    return maps


def kernel(**inputs):
    if "nc" not in _NC_CACHE:
        _NC_CACHE["nc"] = build_program(DEPTH)
    nc = _NC_CACHE["nc"]
    maps = _in_maps(inputs, list(range(8)))
    res = run_bass_kernel_spmd(nc, maps, core_ids=list(range(8)))
    out = np.stack([np.ascontiguousarray(r["yT"].T) for r in res.results], axis=0)
    return out.astype(np.float32)
```

```python
import math
from contextlib import ExitStack
import numpy as np
import concourse.bass as bass
import concourse.mybir as mybir
from concourse.bass_utils import run_bass_kernel_spmd

F32 = mybir.dt.float32
BF16 = mybir.dt.bfloat16
ALU = mybir.AluOpType
AF = mybir.ActivationFunctionType
AX = mybir.AxisListType

D = 1024
S = 2048
NTT = 4
KC = 8
DFF = 2816
NCP = 22
DEPTH = 4
INC = 6656
OFF_CB, OFF_CC, OFF_CV, OFF_Q, OFF_K, OFF_V, OFF_QM, OFF_GA, OFF_GB, OFF_GC = (
    0, 512, 1024, 1536, 2048, 2560, 3072, 3584, 4608, 5632)
EBW = 1920
NPRM = 252
P_GPRE, P_GPOST, P_GFFN, P_GPFFN, P_GMEM, P_BG, P_CW, P_FW, P_FB = 0, 8, 16, 24, 32, 40, 64, 76, 208
NEGM = -30000.0
EPS = 1e-6
C_ONES, C_ID, C_E, C_AGG, C_END = 0, 128, 256, 1280, 1792
GELU_FUNC = [AF.Gelu_apprx_tanh]
USE_FAST_RECIP = [False]


def _esize(dtp):
    return 4 if dtp == F32 else 2


class Sched:
    ENGS = ("pe", "act", "dve", "pool", "sp")
    NDMA = 8
    GRAN = 64

    def __init__(self, nc, stack):
        self.nc = nc
        self.sem = {e: stack.enter_context(nc.semaphore("s_" + e)) for e in self.ENGS}
        self.dsem = {e: [stack.enter_context(nc.semaphore("d_%s%d" % (e, i)))
                         for i in range(self.NDMA)] for e in ("sp", "pool")}
        self.ndma = {e: 0 for e in ("sp", "pool")}
        self.ops = {e: [] for e in self.ENGS}
        self.seen = {e: {} for e in self.ENGS}
        self.gw = {}
        self.gr = {}
        self.track = {}

    def register(self, name, nbytes):
        n = (nbytes + self.GRAN - 1) // self.GRAN
        self.track[name] = nbytes
        self.gw[name] = [None] * n
        self.gr[name] = [None] * n

    def _range(self, ap):
        name = ap.name
        if name not in self.track:
            return None
        pat = ap.ap
        pstep = pat[0][0]
        es = _esize(ap.dtype)
        off = ap.offset % pstep if pstep else ap.offset
        ext = 1
        for st_, cn in pat[1:]:
            ext += (cn - 1) * abs(st_)
        lo = off * es
        hi = (off + ext) * es
        return name, lo // self.GRAN, (hi + self.GRAN - 1) // self.GRAN

    def _need(self, eng, ev, waits):
        if ev is None:
            return
        if ev[0] == "c":
            _, e, idx = ev
            if e == eng and e == "pe":
                return
            key = ("c", e)
            if self.seen[eng].get(key, -1) >= idx:
                return
            self.seen[eng][key] = idx
            self.ops[e][idx]["sig"] = True
            waits.append(ev)
        else:
            _, q, n = ev
            key = ("d", q, n % self.NDMA)
            if self.seen[eng].get(key, -1) >= n:
                return
            self.seen[eng][key] = n
            waits.append(ev)

    def op(self, eng, fn, r=(), w=(), dma=False):
        waits = []
        idx = len(self.ops[eng])
        rr = []
        ww = []
        for ap in r:
            g = self._range(ap)
            if g is None:
                continue
            (ww if g[0].startswith("ps") else rr).append(g)
        for ap in w:
            g = self._range(ap)
            if g is not None:
                ww.append(g)
        for name, a, b in rr:
            gw = self.gw[name]
            last = None
            for i in range(a, b):
                wv = gw[i]
                if wv is not last:
                    self._need(eng, wv, waits)
                    last = wv
        for name, a, b in ww:
            gw, gr = self.gw[name], self.gr[name]
            last = None
            for i in range(a, b):
                wv = gw[i]
                if wv is not last:
                    self._need(eng, wv, waits)
                    last = wv
                rd = gr[i]
                if rd:
                    for k, v in rd.items():
                        if k == "dma":
                            for dv in v:
                                self._need(eng, dv, waits)
                        elif k != eng:
                            self._need(eng, ("c", k, v), waits)
        rec = {"fn": fn, "waits": waits, "sig": False, "dma": None}
        if dma:
            n = self.ndma[eng]
            self.ndma[eng] += 1
            if n >= self.NDMA:
                self._need(eng, ("d", eng, n - self.NDMA), waits)
            rec["dma"] = n
            ev = ("d", eng, n)
        else:
            ev = ("c", eng, idx)
        self.ops[eng].append(rec)
        for name, a, b in rr:
            gr = self.gr[name]
            for i in range(a, b):
                rd = gr[i]
                if rd is None:
                    rd = gr[i] = {}
                if dma:
                    rd.setdefault("dma", []).append(ev)
                else:
                    rd[eng] = idx
        for name, a, b in ww:
            gw, gr = self.gw[name], self.gr[name]
            for i in range(a, b):
                gw[i] = ev
                gr[i] = None
        return ev

    def emit(self, block, final_events=()):
        fw = []
        for ev in final_events:
            self._need("sp", ev, fw)
        self.ops["sp"].append({"fn": None, "waits": fw, "sig": False, "dma": None})
        cnt = {}
        for e in self.ENGS:
            c = 0
            lst = []
            for rec in self.ops[e]:
                if rec["sig"]:
                    c += 1
                lst.append(c)
            cnt[e] = lst
        hsem, dsem, ND = self.sem, self.dsem, self.NDMA

        def run(e, h):
            for rec in self.ops[e]:
                for ev in rec["waits"]:
                    if ev[0] == "c":
                        h.wait_ge(hsem[ev[1]], cnt[ev[1]][ev[2]])
                    else:
                        h.wait_ge(dsem[ev[1]][ev[2] % ND], 16 * (ev[2] // ND + 1))
                if rec["fn"] is None:
                    continue
                ins = rec["fn"](h)
                if rec["dma"] is not None:
                    ins.then_inc(dsem[e][rec["dma"] % ND], 16)
                elif rec["sig"]:
                    ins.then_inc(hsem[e], 1)

        @block.tensor
        def _(h):
            run("pe", h)

        @block.scalar
        def _(h):
            run("act", h)

        @block.vector
        def _(h):
            run("dve", h)

        @block.gpsimd
        def _(h):
            run("pool", h)

        @block.sync
        def _(h):
            run("sp", h)


def build_program(NL=DEPTH, dbg=None, stop_after=None):
    nc = bass.Bass("TRN2", target_bir_lowering=False)
    dt = nc.dram_tensor
    xT_d = dt("xT", [D, S], F32, kind="ExternalInput").ap()
    memT_d = dt("memT", [D, 256], F32, kind="ExternalInput").ap()
    prm_d = dt("prm", [128, DEPTH * NPRM], F32, kind="ExternalInput").ap()
    tbl_d = dt("tbl", [8, 128, EBW], F32, kind="ExternalInput").ap()
    b31_d = dt("b31", [128, 8], F32, kind="ExternalInput").ap()
    cst_d = dt("cst", [128, C_END], F32, kind="ExternalInput").ap()
    w_in_d = dt("w_in", [DEPTH, D, INC], F32, kind="ExternalInput").ap()
    w_co_d = dt("w_conv_out", [DEPTH, 512, D], F32, kind="ExternalInput").ap()
    w_ao_d = dt("w_attn_out", [DEPTH, 512, D], F32, kind="ExternalInput").ap()
    w_kv_d = dt("w_mem_kv", [DEPTH, D, 1024], F32, kind="ExternalInput").ap()
    w_mo_d = dt("w_mem_out", [DEPTH, 512, D], F32, kind="ExternalInput").ap()
    w_out_d = dt("w_out", [DEPTH, D, D], F32, kind="ExternalInput").ap()
    w_up_d = dt("w_up", [DEPTH, D, 2 * DFF], F32, kind="ExternalInput").ap()
    w_dn_d = dt("w_down", [DEPTH, DFF, D], F32, kind="ExternalInput").ap()
    yT_d = dt("yT", [D, S], F32, kind="ExternalOutput").ap()
    ebx_d = dt("ebx", [8, 128, EBW], BF16, kind="Internal").ap()
    dbg_d = {}
    if dbg:
        for name, shape in dbg.items():
            dbg_d[name] = dt("dbg_" + name, list(shape), F32, kind="ExternalOutput").ap()

    with ExitStack() as st:
        sc = Sched(nc, st)

        def sb(name, shape, dtp):
            t = st.enter_context(nc.sbuf_tensor(name, shape, dtp))
            n = 1
            for s_ in shape[1:]:
                n *= s_
            sc.register(name, n * _esize(dtp))
            return t

        xT = sb("xTs", [128, KC, S], F32)
        hTt = sb("hTs", [128, KC * S], BF16)
        Rt = sb("Rs", [128, 32768], BF16)
        EBt = sb("EBs", [128, 2, EBW], BF16)
        wbt = sb("wbs", [128, 8448], BF16)
        ARt = sb("ARs", [128, 5632], BF16)
        prm = sb("prms", [128, DEPTH * NPRM], F32)
        cst = sb("csts", [128, C_END], BF16)
        b31 = sb("b31s", [128, 8], F32)
        epsb = sb("epsb", [128, 1], F32)
        ksum = sb("ksums", [128, 4, 8], F32)
        dK32 = sb("dK32s", [128, 4, 64], F32)
        dKh = sb("dKhs", [128, 4, 128], BF16)
        dKl = sb("dKls", [128, 4, 128], BF16)
        hal = sb("hals", [128, 2 * NCP, 2], BF16)
        ps = []
        for i in range(8):
            ps.append(st.enter_context(nc.psum_tensor("ps%d" % i, [128, 512], F32)))
            sc.register("ps%d" % i, 2048)
        block = st.enter_context(nc.Block())

        hT = hTt[:, :].rearrange("p (c t) -> p c t", c=KC)
        R = Rt[:, :]
        AR = ARt[:, :]
        ones = cst[:, C_ONES:C_ONES + 128]
        ident = cst[:, C_ID:C_ID + 128]
        final_events = []

        def v3(ap2, c):
            return ap2.rearrange("p (c t) -> p c t", c=c)

        def isap(x):
            return not isinstance(x, (int, float)) and x is not None

        def MM(out, lhsT, rhs, start, stop, skip=False):
            if skip:
                sc.op("pe", lambda h: h.matmul(out, lhsT, rhs, start=start, stop=stop, skip_group_check=True),
                      r=[lhsT, rhs], w=[out])
            else:
                sc.op("pe", lambda h: h.matmul(out, lhsT, rhs, start=start, stop=stop), r=[lhsT, rhs], w=[out])

        def ACT(out, in_, func, bias=None, scale=1.0):
            r = [in_] + ([bias] if isap(bias) else [])
            if bias is None:
                sc.op("act", lambda h: h.activation(out=out, in_=in_, func=func, scale=scale), r=r, w=[out])
            else:
                sc.op("act", lambda h: h.activation(out=out, in_=in_, func=func, bias=bias, scale=scale), r=r, w=[out])

        def TT(eng, out, in0, in1, op):
            sc.op(eng, lambda h: h.tensor_tensor(out=out, in0=in0, in1=in1, op=op), r=[in0, in1], w=[out])

        def TS(eng, out, in0, s1, s2, op0, op1=None):
            r = [in0] + [x for x in (s1, s2) if isap(x)]
            if op1 is None:
                sc.op(eng, lambda h: h.tensor_scalar(out=out, in0=in0, scalar1=s1, scalar2=None, op0=op0), r=r, w=[out])
            else:
                sc.op(eng, lambda h: h.tensor_scalar(out=out, in0=in0, scalar1=s1, scalar2=s2, op0=op0, op1=op1),
                      r=r, w=[out])

        def STT(out, in0, scalar, in1, op0, op1):
            r = [in0, in1] + ([scalar] if isap(scalar) else [])
            sc.op("dve", lambda h: h.scalar_tensor_tensor(out=out, in0=in0, scalar=scalar, in1=in1, op0=op0, op1=op1),
                  r=r, w=[out])

        def COPY(eng, out, in_):
            if eng == "act":
                ACT(out, in_, AF.Copy)
            else:
                sc.op(eng, lambda h: h.tensor_copy(out=out, in_=in_), r=[in_], w=[out])

        def RECIP(out, in_):
            sc.op("dve", lambda h: h.reciprocal(out=out, in_=in_), r=[in_], w=[out])

        def RECIPF(out, in_):
            if USE_FAST_RECIP[0]:
                sc.op("dve", lambda h: h.reciprocal_approx_fast(out=out, in_=in_), r=[in_], w=[out])
            else:
                RECIP(out, in_)

        def MEMSET(eng, out, val):
            sc.op(eng, lambda h: h.memset(out, val), w=[out])

        def DMA(q, out, in_):
            return sc.op(q, lambda h: h.dma_start(out=out, in_=in_), r=[in_], w=[out], dma=True)

        wstate = {"n": 0, "n3": 0}

        def wload(parts, three=False):
            if three:
                base = (wstate["n3"] % 3) * 2816
                wstate["n3"] += 1
            else:
                base = (wstate["n"] % 2) * 4096
                wstate["n"] += 1
            for (off, kc, ncols, src) in parts:
                DMA("pool", wbt[:, base + off:base + off + kc * ncols].rearrange("p (k n) -> p k n", k=kc), src)
            return base

        def wview(base, off, kc, ncols):
            return wbt[:, base + off:base + off + kc * ncols].rearrange("p (k n) -> p k n", k=kc)

        def wsrc(w_d, l, nk, c0, ncols):
            return w_d[l, 0:nk * 128, c0:c0 + ncols].rearrange("(k p) n -> p k n", p=128)

        steps = []

        preissued = []

        def run_steps(depth=1, next_ld=None):
            n = len(steps)
            slots = [None] * n
            issued = 0
            if preissued and n and steps[0][0] is not None:
                slots[0] = preissued.pop(0)
                issued = 1

            def issue_upto(k):
                nonlocal issued
                while issued < min(k + 1, n):
                    if steps[issued][0] is not None:
                        slots[issued] = steps[issued][0]()
                    issued += 1
            for i, (ld, comp) in enumerate(steps):
                issue_upto(i + depth)
                if i == n - 1 and next_ld is not None:
                    preissued.append(next_ld())
                comp(slots[i])
            steps.clear()

        rot = {"n": 0}

        def next_ps(lo=0, n=8):
            i = lo + rot["n"] % n
            rot["n"] += 1
            return i

        def pcol(l, off):
            return prm[:, l * NPRM + off:l * NPRM + off + 1]

        def ar_bf(off, n):
            return AR[:, off:off + n]

        def ar_f32(off, n):
            return AR[:, off:off + n].bitcast(F32)

        def tsl(tt):
            return slice(tt * 512, (tt + 1) * 512)

        sc.register("ebx", EBW * 2)
        for h_ in range(8):
            DMA("pool", EBt[:, h_ % 2, :], tbl_d[h_])
            ACT(EBt[:, h_ % 2, :], EBt[:, h_ % 2, :], AF.Exp)
            DMA("sp", ebx_d[h_], EBt[:, h_ % 2, :])
        DMA("sp", prm[:, :], prm_d)
        DMA("sp", b31[:, :], b31_d)
        DMA("pool", cst[:, :], cst_d)
        MEMSET("dve", epsb[:, :], EPS)
        for c in range(KC):
            DMA("sp", xT[:, c, :], xT_d[c * 128:(c + 1) * 128, :])

        def rstd_from_ps(psi, ncol, rs_ap, rstd_ap):
            ACT(rs_ap, ps[psi][:, 0:ncol], AF.Ln, bias=epsb[:, 0:1], scale=1.0 / D)
            ACT(rstd_ap, rs_ap, AF.Exp, scale=-0.5)

        nk_ = {"n": 0}
        EBf = EBt[:, :, :].rearrange("p a b -> p (a b)")
        tmpA = EBf[:, 0:1024].bitcast(F32)
        rstdR = [EBf[:, 1024:2048].bitcast(F32), EBf[:, 2048:3072].bitcast(F32)]
        NSQ = [ar_bf(3584, 512), ar_bf(4096, 512)]
        NRSTD = ar_f32(4608, 1024)

        def rstd_ps(bank, rstd):
            ACT(rstd, ps[bank][:, :], AF.Ln, bias=epsb[:, 0:1], scale=1.0 / D)
            ACT(rstd, rstd, AF.Exp, scale=-0.5)

        def norm_tt(l, goff, tt):
            psi = next_ps(0, 4)
            for c in range(KC):
                k = nk_["n"]
                nk_["n"] += 1
                ACT(NSQ[k % 2], xT[:, c, tsl(tt)], AF.Square)
                MM(ps[psi][:, :], ones, NSQ[k % 2], c == 0, c == KC - 1)
            rstd_ps(psi, NRSTD)
            for c in range(KC):
                STT(hT[:, c, tsl(tt)], xT[:, c, tsl(tt)], pcol(l, goff + c), NRSTD, ALU.mult, ALU.mult)

        def resid_apply(l, goff, tt, ov, rstd, tmps):
            for c in range(KC):
                k = nk_["n"]
                nk_["n"] += 1
                tmp = tmps[k % len(tmps)]
                STT(tmp, ov(c), pcol(l, goff + c), rstd, ALU.mult, ALU.mult)
                TT("pool", xT[:, c, tsl(tt)], xT[:, c, tsl(tt)], tmp, ALU.add)

        dq = []

        def make_pieces(l_res, goff_res, tt, ov, rstd, l_norm, goff_norm):
            P = []
            for c in range(KC):
                def rp(c=c):
                    STT(tmpA, ov(c), pcol(l_res, goff_res + c), rstd, ALU.mult, ALU.mult)
                    TT("pool", xT[:, c, tsl(tt)], xT[:, c, tsl(tt)], tmpA, ALU.add)
                P.append(rp)
            if l_norm is not None:
                def stats():
                    psi = next_ps(0, 4)
                    for c in range(KC):
                        k = nk_["n"]
                        nk_["n"] += 1
                        ACT(NSQ[k % 2], xT[:, c, tsl(tt)], AF.Square)
                        MM(ps[psi][:, :], ones, NSQ[k % 2], c == 0, c == KC - 1)
                    rstd_ps(psi, NRSTD)
                P.append(stats)
                for c in range(KC):
                    P.append(lambda c=c: STT(hT[:, c, tsl(tt)], xT[:, c, tsl(tt)], pcol(l_norm, goff_norm + c),
                                             NRSTD, ALU.mult, ALU.mult))
            return P

        def drain(n):
            for _ in range(min(n, len(dq))):
                dq.pop(0)()

        Vv = R[:, 0:8192].rearrange("p (i c) -> p i c", i=16)
        QT = v3(R[:, 8192:16384], 4)
        KT = v3(R[:, 16384:24576], 4)
        Vh = [R[:, 24576 + s * 2048:24576 + (s + 1) * 2048].rearrange("p (i c) -> p i c", i=16)
              for s in range(2)]
        QTm = [R[:, 28672 + s * 2048:28672 + (s + 1) * 2048] for s in range(2)]
        early = {}

        def ld_cols_l(l, c0):
            return lambda: wload([(0, KC, 512, wsrc(w_in_d, l, KC, c0, 512))])

        def proj_T(s, dstT, tts=(0, 1, 2, 3)):
            W = wview(s, 0, KC, 512)
            for j in range(4):
                for tt in tts:
                    psi = next_ps()
                    for kc in range(KC):
                        MM(ps[psi][:, :], W[:, kc, j * 128:(j + 1) * 128], hT[:, kc, tsl(tt)],
                           kc == 0, kc == KC - 1)
                    COPY("act", dstT[:, j, tsl(tt)], ps[psi][:, :])

        def k_stats():
            for j in range(4):
                sc.op("dve", lambda h, j=j: h.tensor_reduce(
                    out=ksum[:, j, :], in_=KT[:, j, :].rearrange("p (n k) -> p n k", n=8),
                    axis=AX.X, op=ALU.add), r=[KT[:, j, :]], w=[ksum[:, j, :]])
                TT("dve", dK32[:, j, :].rearrange("p (n m) -> p n m", n=8),
                   ksum[:, j, :].unsqueeze(1).broadcast_to([128, 8, 8]),
                   ksum[:, j, :].unsqueeze(2).broadcast_to([128, 8, 8]), ALU.subtract)
                COPY("dve", dKh[:, j, 0:64], dK32[:, j, :])
                TT("dve", dKl[:, j, 0:64], dK32[:, j, :], dKh[:, j, 0:64], ALU.subtract)
                COPY("pool", dKh[:, j, 64:128], dKh[:, j, 0:64])
                COPY("pool", dKl[:, j, 64:128], dKl[:, j, 0:64])

        def v_proj(s, irange):
            W = wview(s, 0, KC, 512)
            for i in irange:
                psi = next_ps()
                for kc in range(KC):
                    MM(ps[psi][:, :], hT[:, kc, i * 128:(i + 1) * 128], W[:, kc, :], kc == 0, kc == KC - 1)
                COPY("act", Vv[:, i, :], ps[psi][:, :])

        def early_proj_steps(l2):
            def ka(s):
                early["K"] = s
                proj_T(s, KT, (0, 1))

            def va(s):
                early["V"] = s
                v_proj(s, range(8))
            return [(ld_cols_l(l2, OFF_K), ka), (ld_cols_l(l2, OFF_V), va)]

        def dump(name, view_fn, nchunk, ncol):
            tmpo = ar_f32(4096, 1024)
            for c in range(nchunk):
                for t0 in range(0, ncol, 512):
                    n = min(512, ncol - t0)
                    COPY("dve", tmpo[:, 0:n], view_fn(c, t0, n))
                    final_events.append(DMA("sp", dbg_d[name][c * 128:(c + 1) * 128, t0:t0 + n], tmpo[:, 0:n]))

        for l in range(NL):
            last = (l == NL - 1)
            if l == 0:
                for tt in range(NTT):
                    norm_tt(0, P_GPRE, tt)
            if dbg and "hT" in dbg and last:
                dump("hT", lambda c, t0, n: hT[:, c, t0:t0 + n], KC, S)

            def ld_cols(c0):
                return ld_cols_l(l, c0)

            def gate_ld(gate_off, w_bo_d, q4):
                return lambda: wload([(0, KC, 256, wsrc(w_in_d, l, KC, gate_off + q4 * 256, 256)),
                                      (2048, 4, 256, wsrc(w_bo_d, l, 4, q4 * 256, 256))])

            def ld_wo(g2):
                return lambda: wload([(0, KC, 512, wsrc(w_out_d, l, KC, g2 * 512, 512))])

            def ld_up(cp0):
                return lambda: wload([(0, KC, 256, wsrc(w_up_d, l, KC, cp0 * 128, 256)),
                                      (2048, KC, 256, wsrc(w_up_d, l, KC, DFF + cp0 * 128, 256))])

            if l == 0:
                def comp_K(s):
                    proj_T(s, KT)
                    k_stats()
                steps.append((ld_cols(OFF_K), comp_K))
                steps.append((ld_cols(OFF_V), lambda s: v_proj(s, range(16))))
            else:
                def comp_Kb(_s):
                    proj_T(early["K"], KT, (2, 3))
                    k_stats()
                steps.append((None, comp_Kb))
                steps.append((None, lambda _s: v_proj(early["V"], range(8, 16))))
            steps.append((ld_cols(OFF_Q), lambda s: proj_T(s, QT)))
            run_steps(next_ld=None if stop_after in ("proj", "att") else gate_ld(OFF_GB, w_ao_d, 0))
            if dbg and "QT" in dbg and last:
                dump("QT", lambda c, t0, n: QT[:, c, t0:t0 + n], 4, S)
            if dbg and "KT" in dbg and last:
                dump("KT", lambda c, t0, n: KT[:, c, t0:t0 + n], 4, S)
            if stop_after == "proj":
                break

            LA, NSB, NPB = 4, 5, 6
            Pt = [ar_bf(i * 512, 512) for i in range(NPB)]
            ind = ar_bf(3072, 512)
            negs = [ar_bf(3584, 512), ar_bf(4096, 512)]
            rec = ar_f32(4608, 1024)

            def head_prep(h_):
                e = h_ % 2
                jc, hp = h_ // 2, (h_ % 2) * 64
                DMA("sp", EBt[:, e, :], ebx_d[h_])
                voff, ooff = (0, 64) if e == 0 else (64, 0)
                zp = 64 - hp
                if h_ < 2:
                    MEMSET("pool", Vh[e][:, :, ooff:ooff + 64], 1.0)
                    MEMSET("pool", QTm[e][zp:zp + 64, :], 0.0)
                COPY("dve", Vh[e][:, :, voff:voff + 64], Vv[:, :, h_ * 64:(h_ + 1) * 64])
                COPY("dve", QTm[e][hp:hp + 64, :], QT[hp:hp + 64, jc, :])

            def sel_prep1(h_, j):
                e, jc = h_ % 2, h_ // 2
                qsl = QTm[e][:, j * 512:(j + 1) * 512]
                MM(ps[0][:, :], dKh[:, jc, :], qsl, True, False)
                MM(ps[0][:, :], dKl[:, jc, :], qsl, False, True)
                TS("dve", ind, ps[0][:, :], 0.0, None, ALU.is_gt)

            def sel_prep2(h_, j):
                ng = negs[j % 2]
                for hf in range(2):
                    qb = 2 * j + hf
                    MM(ps[0][:, hf * 256:(hf + 1) * 256],
                       cst[:, C_AGG + (qb - 4) * 128:C_AGG + (qb - 3) * 128],
                       ind[:, hf * 256:(hf + 1) * 256], True, True, skip=True)
                TS("dve", ng, ps[0][:, :], 2.5, NEGM, ALU.is_ge, ALU.mult)

            tiles = [(h_, j, kt) for h_ in range(8) for j in range(NTT) for kt in range(4 * j + 4)]
            NT_ = len(tiles)

            def stageA(t):
                h_, j, kt = tiles[t]
                e, jc = h_ % 2, h_ // 2
                if (j, kt) == (0, 0):
                    sel_prep1(h_, 2)
                elif (j, kt) == (1, 2):
                    sel_prep2(h_, 2)
                elif (j, kt) == (1, 6):
                    sel_prep1(h_, 3)
                elif (j, kt) == (2, 4):
                    sel_prep2(h_, 3)
                q0, k0, n = j * 512, kt * 128, kt // 2
                qsl = QTm[e][:, q0:q0 + 512]
                pS = 1 + t % NSB
                use_mask = j >= 2 and n <= 2 * j
                c0 = 256 if kt >= 4 * j + 2 else 0
                MM(ps[pS][:, c0:512], KT[:, jc, k0:k0 + 128], qsl[:, c0:512], True, not use_mask)
                if use_mask:
                    MM(ps[pS][:, :], cst[:, C_E + n * 128:C_E + (n + 1) * 128], negs[j % 2], False, True)
                pt = Pt[t % NPB]
                delta = q0 - k0
                if delta >= 1024:
                    ACT(pt, ps[pS][:, :], AF.Exp, bias=b31[:, h_:h_ + 1], scale=0.125)
                else:
                    ACT(pt[:, c0:512], ps[pS][:, c0:512], AF.Exp, scale=0.125)
                    u0 = delta + 512
                    TT("dve" if t % 2 == 0 else "pool", pt[:, c0:512], pt[:, c0:512],
                       EBt[:, e, u0 + c0:u0 + 512], ALU.mult)

            def stageC(t):
                h_, j, kt = tiles[t]
                e, jc = h_ % 2, h_ // 2
                nk = 4 * j + 4
                pO = 6 + (h_ * NTT + j) % 2
                c0 = 256 if kt >= 4 * j + 2 else 0
                MM(ps[pO][:, c0:512], Vh[e][:, kt, :], Pt[t % NPB][:, c0:512], kt == 0, kt == nk - 1,
                   skip=(c0 != 0))
                if kt == nk - 1:
                    vr = slice(0, 64) if e == 0 else slice(64, 128)
                    sr = slice(64, 128) if e == 0 else slice(0, 64)
                    ACT(rec[vr, :], ps[pO][sr, :], AF.Ln)
                    ACT(rec[vr, :], rec[vr, :], AF.Exp, scale=-1.0)
                    TT("dve", QT[vr, jc, j * 512:(j + 1) * 512], ps[pO][vr, :], rec[vr, :], ALU.mult)
                    if j == NTT - 1 and h_ + 2 < 8:
                        head_prep(h_ + 2)

            head_prep(0)
            head_prep(1)
            for t in range(NT_ + LA):
                if t < NT_:
                    stageA(t)
                if t >= LA:
                    stageC(t - LA)
            OT = QT
            if dbg and "OT" in dbg and last:
                dump("OT", lambda c, t0, n: OT[:, c, t0:t0 + n], 4, S)
            if stop_after == "att":
                break

            mrg = v3(R[:, 16384:32768], KC)
            sg = [ar_bf(0, 512), ar_bf(512, 512)]
            gtmp = [ar_bf(1024, 512), ar_bf(1536, 512)]
            gk = {"n": 0}

            def gate_branch(bidx, gate_off, w_bo_d, srcT, first):
                for q4 in range(4):
                    ld = gate_ld(gate_off, w_bo_d, q4)

                    def comp(s, q4=q4):
                        Wg = wview(s, 0, KC, 256)
                        Wb = wview(s, 2048, 4, 256)
                        for o2 in range(2):
                            oc = q4 * 2 + o2
                            for tt in range(NTT):
                                pg = next_ps()
                                for kc in range(KC):
                                    MM(ps[pg][:, :], Wg[:, kc, o2 * 128:(o2 + 1) * 128], hT[:, kc, tsl(tt)],
                                       kc == 0, kc == KC - 1)
                                k = gk["n"]
                                gk["n"] += 1
                                ACT(sg[k % 2], ps[pg][:, :], AF.Sigmoid, bias=pcol(l, P_BG + bidx * 8 + oc))
                                py = next_ps()
                                for kc in range(4):
                                    MM(ps[py][:, :], Wb[:, kc, o2 * 128:(o2 + 1) * 128], srcT[:, kc, tsl(tt)],
                                       kc == 0, kc == 3)
                                dst = mrg[:, oc, tsl(tt)]
                                if first:
                                    TT("dve", dst, ps[py][:, :], sg[k % 2], ALU.mult)
                                else:
                                    TT("dve", gtmp[k % 2], ps[py][:, :], sg[k % 2], ALU.mult)
                                    TT("pool", dst, dst, gtmp[k % 2], ALU.add)
                    steps.append((ld, comp))

            gate_branch(1, OFF_GB, w_ao_d, OT, True)
            run_steps(next_ld=None if stop_after == "brB" else ld_cols(OFF_CC))
            if stop_after == "brB":
                if dbg and "mrg" in dbg and last:
                    dump("mrg", lambda c, t0, n: mrg[:, c, t0:t0 + n], KC, S)
                break

            ccT = v3(R[:, 0:8192], 4)
            zT = v3(R[:, 8192:16384], 4)
            dgm = ar_bf(2048, 1536).rearrange("p (a b) -> p a b", a=12)
            for j in range(4):
                for tap in range(3):
                    TS("dve", dgm[:, j * 3 + tap, :], ident, pcol(l, P_CW + tap * 4 + j), None, ALU.mult)

            def comp_cc(s):
                proj_T(s, ccT)

            def comp_cv(s):
                W = wview(s, 0, KC, 512)
                for j in range(4):
                    for tt in range(NTT):
                        psi = next_ps()
                        for kc in range(KC):
                            MM(ps[psi][:, :], W[:, kc, j * 128:(j + 1) * 128], hT[:, kc, tsl(tt)],
                               kc == 0, kc == KC - 1)
                        TT("dve", zT[:, j, tsl(tt)], ps[psi][:, :], ccT[:, j, tsl(tt)], ALU.mult)

            def comp_cb(s):
                W = wview(s, 0, KC, 512)
                for j in range(4):
                    for tt in range(NTT):
                        pc = next_ps()
                        t0 = tt * 512
                        for tap in range(3):
                            sh = 2 - tap
                            lhs = dgm[:, j * 3 + tap, :]
                            if sh == 0:
                                MM(ps[pc][:, :], lhs, zT[:, j, t0:t0 + 512], tap == 0, True, skip=True)
                            elif tt == 0:
                                MM(ps[pc][:, sh:512], lhs, zT[:, j, 0:512 - sh], tap == 0, False, skip=True)
                            else:
                                MM(ps[pc][:, :], lhs, zT[:, j, t0 - sh:t0 - sh + 512], tap == 0, False, skip=True)
                        COPY("act", ccT[:, j, tsl(tt)], ps[pc][:, :])
                        psi = next_ps()
                        for kc in range(KC):
                            MM(ps[psi][:, :], W[:, kc, j * 128:(j + 1) * 128], hT[:, kc, tsl(tt)],
                               kc == 0, kc == KC - 1)
                        TT("dve", ccT[:, j, tsl(tt)], ps[psi][:, :], ccT[:, j, tsl(tt)], ALU.mult)

            qmT = v3(R[:, 0:8192], 4)
            Y = R[:, 8192:16384]
            memf = Y[:, 0:4096].bitcast(F32).rearrange("p (c m) -> p c m", c=KC)
            memn = Y[:, 4096:6144].rearrange("p (c m) -> p c m", c=KC)
            mkT = Y[:, 6144:7168].rearrange("p (c m) -> p c m", c=4)
            mv = Y[:, 7168:8192].rearrange("p (i c) -> p i c", i=2)
            msq8 = [ar_bf(3584 + 256 * c, 256) for c in range(KC)]
            mrs = EBf[:, 3072:3584].bitcast(F32)
            mrstd = EBf[:, 0:512].bitcast(F32)

            def mem_prep(_s):
                DMA("sp", memf, memT_d.rearrange("(c p) m -> p c m", p=128))
                for c in range(KC):
                    ACT(msq8[c], memf[:, c, :], AF.Square)

            def mem_prep2(_s):
                pm = next_ps()
                for c in range(KC):
                    MM(ps[pm][:, 0:256], ones, msq8[c], c == 0, c == KC - 1)
                rstd_from_ps(pm, 256, mrs, mrstd)
                for c in range(KC):
                    STT(memn[:, c, :], memf[:, c, :], pcol(l, P_GMEM + c), mrstd, ALU.mult, ALU.mult)

            steps.append((ld_cols(OFF_CC), comp_cc))
            steps.append((ld_cols(OFF_CV), comp_cv))
            steps.append((ld_cols(OFF_CB), comp_cb))
            steps.append((None, mem_prep))
            gate_branch(0, OFF_GA, w_co_d, ccT, False)
            steps.insert(len(steps) - 3, (None, mem_prep2))
            run_steps(next_ld=None if stop_after == "brA" else ld_cols(OFF_QM))
            if stop_after == "brA":
                if dbg and "mrg" in dbg and last:
                    dump("mrg", lambda c, t0, n: mrg[:, c, t0:t0 + n], KC, S)
                break

            def ld_kv(c0):
                return lambda: wload([(0, KC, 512, wsrc(w_kv_d, l, KC, c0, 512))])

            def comp_mk(s):
                W = wview(s, 0, KC, 512)
                for j in range(4):
                    psi = next_ps()
                    for kc in range(KC):
                        MM(ps[psi][:, 0:256], W[:, kc, j * 128:(j + 1) * 128], memn[:, kc, :],
                           kc == 0, kc == KC - 1)
                    COPY("act", mkT[:, j, :], ps[psi][:, 0:256])

            def comp_mv(s):
                W = wview(s, 0, KC, 512)
                for i in range(2):
                    psi = next_ps()
                    for kc in range(KC):
                        MM(ps[psi][:, :], memn[:, kc, i * 128:(i + 1) * 128], W[:, kc, :], kc == 0, kc == KC - 1)
                    COPY("act", mv[:, i, :], ps[psi][:, :])

            def comp_qm(s):
                proj_T(s, qmT)

            def mem_attn():
                NMP = 4
                mP = [ar_bf(i * 512, 512) for i in range(NMP)]
                mrec = ar_f32(3072, 1024)
                mt = [(hm, tt, i) for hm in range(4) for tt in range(NTT) for i in range(2)]

                def mA(t):
                    hm, tt, i = mt[t]
                    pS = 4 + t % 4
                    MM(ps[pS][:, :], mkT[:, hm, i * 128:(i + 1) * 128], qmT[:, hm, tsl(tt)], True, True)
                    ACT(mP[t % NMP], ps[pS][:, :], AF.Exp, scale=128.0 ** -0.5)

                def mC(t):
                    hm, tt, i = mt[t]
                    g = t // 2
                    pC, pSm = (g % 2) * 2, (g % 2) * 2 + 1
                    MM(ps[pC][:, :], mv[:, i, hm * 128:(hm + 1) * 128], mP[t % NMP], i == 0, i == 1)
                    MM(ps[pSm][:, :], ones, mP[t % NMP], i == 0, i == 1)
                    if i == 1:
                        ACT(mrec, ps[pSm][:, :], AF.Ln)
                        ACT(mrec, mrec, AF.Exp, scale=-1.0)
                        TT("dve", qmT[:, hm, tsl(tt)], ps[pC][:, :], mrec, ALU.mult)

                for t in range(len(mt) + 2):
                    if t < len(mt):
                        mA(t)
                    if t >= 2:
                        mC(t - 2)

            def comp_mv_att(s):
                comp_mv(s)
                mem_attn()

            steps.append((ld_cols(OFF_QM), comp_qm))
            steps.append((ld_kv(0), comp_mk))
            steps.append((ld_kv(512), comp_mv_att))
            gate_branch(2, OFF_GC, w_mo_d, qmT, False)
            run_steps(next_ld=None if stop_after == "brC" else ld_wo(0))
            if dbg and "mrg" in dbg and last:
                dump("mrg", lambda c, t0, n: mrg[:, c, t0:t0 + n], KC, S)
            if stop_after == "brC":
                break

            oT4 = R[:, 0:16384].rearrange("p (t c n) -> p t c n", t=4, c=KC)
            fdg = [ar_bf(s_ * 768, 768).rearrange("p (a b) -> p a b", a=6) for s_ in range(2)]
            gel = [ar_bf(1536, 512), ar_bf(2048, 512)]
            osq = [ar_bf(2560, 512), ar_bf(3072, 512)]
            rstdI = [ar_f32(1536, 1024), ar_f32(2560, 1024)]
            tmpB = ar_f32(0, 1024)
            ok = {"n": 0}
            sspend = []

            wo_slots = {}

            def comp_wo_all(s1):
                Ws = (wview(wo_slots["a"], 0, KC, 512), wview(s1, 0, KC, 512))
                rdst = [rstdI[0], rstdI[1], rstdR[0], rstdR[1]]

                def resid01(tt):
                    resid_apply(l, P_GPOST, tt, lambda c, tt=tt: oT4[:, tt, c, :], rstdI[tt], [tmpA, tmpB])
                for tt in range(NTT):
                    for oc in range(KC):
                        if tt == NTT - 1 and oc == 4 and stop_after != "mix":
                            preissued.append(ld_up(0)())
                        psi = next_ps(0, 4)
                        W = Ws[oc // 4]
                        o4 = oc % 4
                        for kc in range(KC):
                            MM(ps[psi][:, :], W[:, kc, o4 * 128:(o4 + 1) * 128], mrg[:, kc, tsl(tt)],
                               kc == 0, kc == KC - 1)
                        if sspend:
                            sspend.pop()()
                        if oc == 1 and tt >= 1:
                            rstd_ps(4 + tt - 1, rdst[tt - 1])
                            if tt == 1:
                                resid01(0)
                            elif tt == 2:
                                norm_tt(l, P_GFFN, 0)
                                resid01(1)
                            elif tt == 3:
                                norm_tt(l, P_GFFN, 1)
                        sq = osq[ok["n"] % 2]
                        ok["n"] += 1
                        COPY("act", oT4[:, tt, oc, :], ps[psi][:, :])
                        ACT(sq, ps[psi][:, :], AF.Square)
                        sspend.append(lambda sq=sq, tt=tt, oc=oc: MM(ps[4 + tt][:, :], ones, sq, oc == 0, oc == KC - 1))
                sspend.pop()()
                rstd_ps(7, rdst[3])

            steps.append((ld_wo(0), lambda s0: wo_slots.__setitem__("a", s0)))
            steps.append((ld_wo(1), comp_wo_all))
            run_steps()

            def deferred_b1(_s):
                for tt in (2, 3):
                    resid_apply(l, P_GPOST, tt, lambda c, tt=tt: oT4[:, tt, c, :], rstdR[tt - 2], [tmpA])
                    norm_tt(l, P_GFFN, tt)

            def deferred_b2(_s):
                for tt in (0, 1):
                    resid_apply(l, P_GPFFN, tt, lambda c, tt=tt: hT[:, c, tsl(tt)], rstdR[tt], [tmpA])
                    if l + 1 < NL:
                        norm_tt(l + 1, P_GPRE, tt)
            if stop_after == "mix":
                deferred_b1(None)
                break

            for hf in range(2):
                actT = R[:, 0:NCP * 1024].rearrange("p (c t) -> p c t", c=NCP)
                pre = R[:, 22528:22528 + 4104].rearrange("p (a b t) -> p a b t", a=2, b=2)
                fk = {"n": 0}

                pend = []

                def conv_stage(cp, k, sl):
                    def f():
                        for i in range(2):
                            pa, pg = next_ps(4, 4), next_ps(4, 4)
                            for ag, pp in ((0, pa), (1, pg)):
                                for tap in range(3):
                                    MM(ps[pp][:, :], fdg[sl][:, ag * 3 + tap, :],
                                       pre[:, sl, ag, i * 512 + tap:i * 512 + tap + 512], tap == 0, tap == 2)
                            ga = gel[(2 * k + i) % 2]
                            ACT(ga, ps[pa][:, :], GELU_FUNC[0], bias=pcol(l, P_FB + cp))
                            STT(actT[:, cp, tsl(i)], ps[pg][:, :], pcol(l, P_FB + NCP + cp), ga, ALU.add, ALU.mult)
                    return f

                def comp_up(cp0):
                    def f(s):
                        Ws = (wview(s, 0, KC, 256), wview(s, 2048, KC, 256))
                        for c2 in range(2):
                            cp = cp0 + c2
                            k = fk["n"]
                            fk["n"] += 1
                            sl = k % 2
                            for ag in range(2):
                                for tap in range(3):
                                    col = P_FW + tap * 44 + ag * NCP + cp
                                    TS("dve", fdg[sl][:, ag * 3 + tap, :], ident, pcol(l, col), None, ALU.mult)
                            for ag in range(2):
                                if hf == 0:
                                    MEMSET("dve", pre[:, sl, ag, 0:2], 0.0)
                                else:
                                    COPY("dve", pre[:, sl, ag, 0:2], hal[:, ag * NCP + cp, :])
                            for ag in range(2):
                                for i in range(2):
                                    psi = next_ps(0, 4)
                                    for kc in range(KC):
                                        MM(ps[psi][:, :], Ws[ag][:, kc, c2 * 128:(c2 + 1) * 128], hT[:, kc, tsl(2 * hf + i)],
                                           kc == 0, kc == KC - 1)
                                    COPY("act" if ag == 0 else "dve",
                                         pre[:, sl, ag, 2 + i * 512:2 + (i + 1) * 512], ps[psi][:, :])
                            if hf == 0:
                                for ag in range(2):
                                    COPY("dve", hal[:, ag * NCP + cp, :], pre[:, sl, ag, 1024:1026])
                            drain(5 if hf == 0 else 2)
                            if pend:
                                pend.pop()()
                            pend.append(conv_stage(cp, k, sl))
                    return f

                def fill_b1(_s):
                    for tt in (2, 3):
                        dq.extend(make_pieces(l, P_GPOST, tt, lambda c, tt=tt: oT4[:, tt, c, :], rstdR[tt - 2],
                                              l, P_GFFN))

                if hf == 0:
                    fill_b1(None)
                else:
                    for tt in (0, 1):
                        dq.extend(make_pieces(l, P_GPFFN, tt, lambda c, tt=tt: hT[:, c, tsl(tt)], rstdR[tt],
                                              (l + 1) if l + 1 < NL else None, P_GPRE))
                for cp0 in range(0, NCP, 2):
                    if hf == 0 and cp0 == 8:
                        steps.append((None, lambda _s: drain(len(dq))))
                    steps.append((ld_up(cp0), comp_up(cp0)))
                run_steps()
                pend.pop()()
                drain(len(dq))

                def ld_dn(oc):
                    return lambda: wload([(0, NCP, 128, wsrc(w_dn_d, l, NCP, oc * 128, 128))], three=True)

                def comp_dn(oc):
                    def f(s):
                        W = wview(s, 0, NCP, 128)
                        for i in range(2):
                            psi = next_ps(0, 4)
                            for cp in range(NCP):
                                MM(ps[psi][:, :], W[:, cp, :], actT[:, cp, tsl(i)], cp == 0, cp == NCP - 1)
                            if sspend:
                                sspend.pop()()
                            sq = osq[ok["n"] % 2]
                            ok["n"] += 1
                            COPY("act", hT[:, oc, tsl(2 * hf + i)], ps[psi][:, :])
                            ACT(sq, ps[psi][:, :], AF.Square)
                            sspend.append(lambda sq=sq, i=i, oc=oc: MM(ps[4 + i][:, :], ones, sq, oc == 0, oc == KC - 1))
                    return f

                for oc in range(KC):
                    steps.append((ld_dn(oc), comp_dn(oc)))

                def rstd_step(_s):
                    sspend.pop()()
                    rstd_ps(4, rstdR[0])
                    rstd_ps(5, rstdR[1])
                steps.append((None, rstd_step))
                run_steps(depth=2)
                if hf == 1 and l + 1 < NL:
                    steps.extend(early_proj_steps(l + 1))
                    run_steps()
                if hf == 1:
                    for tt in (2, 3):
                        resid_apply(l, P_GPFFN, tt, lambda c, tt=tt: hT[:, c, tsl(tt)], rstdR[tt - 2], [tmpA, tmpB])
                        if l + 1 < NL:
                            norm_tt(l + 1, P_GPRE, tt)

        for c in range(KC):
            final_events.append(DMA("sp", yT_d[c * 128:(c + 1) * 128, :], xT[:, c, :]))
        sc.emit(block, final_events)
    return nc


def _rel_bucket_np(dist):
    n = np.maximum(dist, 0)
    is_small = n < 16
    n_f = np.maximum(n, 16).astype(np.float32)
    v = (np.log(n_f / np.float32(16)) / np.float32(math.log(1024 / 16)) * np.float32(16))
    large = 16 + v.astype(np.int32)
    large = np.minimum(large, 31)
    return np.where(is_small, n, large)


def _host_prep(inp):
    f = np.float32
    prm = np.zeros((128, DEPTH * NPRM), f)
    for l in range(DEPTH):
        o = l * NPRM
        for name, off in (("g_pre_mix", P_GPRE), ("g_post_mix", P_GPOST), ("g_pre_ffn", P_GFFN),
                          ("g_post_ffn", P_GPFFN), ("g_mem", P_GMEM)):
            prm[:, o + off:o + off + 8] = np.asarray(inp[name][l], f).reshape(8, 128).T
        prm[:, o + P_BG:o + P_BG + 24] = np.asarray(inp["b_gate"][l], f).reshape(24, 128).T
        prm[:, o + P_CW:o + P_CW + 12] = np.asarray(inp["conv_mix_w"][l], f).reshape(12, 128).T
        prm[:, o + P_FW:o + P_FW + 132] = np.asarray(inp["ffn_conv_w"][l], f).reshape(132, 128).T
        prm[:, o + P_FB:o + P_FB + 44] = np.asarray(inp["ffn_conv_b"][l], f).reshape(44, 128).T
    rb = np.asarray(inp["rel_bias"], f)
    p = np.arange(128)[:, None]
    u = np.arange(EBW)[None, :]
    dist = u - 512 - p
    bucket = _rel_bucket_np(dist)
    tbl = np.empty((8, 128, EBW), f)
    for h in range(8):
        t = rb[bucket, h]
        tbl[h] = np.where(dist >= 0, t, f(NEGM))
    b31 = np.ascontiguousarray(np.broadcast_to(rb[31][None, :], (128, 8))).astype(f)
    cst = np.zeros((128, C_END), f)
    cst[:, C_ONES:C_ONES + 128] = 1.0
    cst[:, C_ID:C_ID + 128] = np.eye(128, dtype=f)
    for n in range(8):
        cst[n, C_E + n * 128:C_E + (n + 1) * 128] = 1.0
    for qb in range(4, 8):
        for n in range(qb):
            for m in range(qb):
                cst[n * 8 + m, C_AGG + (qb - 4) * 128 + n] = 1.0
    return prm, tbl, b31, cst


_W_NAMES = ("w_in", "w_conv_out", "w_attn_out", "w_mem_kv", "w_mem_out", "w_out", "w_up", "w_down")
_NC_CACHE = {}


def _in_maps(inputs, cores):
    prm, tbl, b31, cst = _host_prep(inputs)
    x = np.asarray(inputs["x"], np.float32)
    mem = np.asarray(inputs["mem"], np.float32)
    ws = {n: np.ascontiguousarray(np.asarray(inputs[n], np.float32)) for n in _W_NAMES}
    maps = []
    for b in cores:
        m = {"xT": np.ascontiguousarray(x[b].T), "memT": np.ascontiguousarray(mem[b].T),
             "prm": prm, "tbl": tbl, "b31": b31, "cst": cst}
        m.update(ws)
        maps.append(m)
    return maps


def kernel(**inputs):
    if "nc" not in _NC_CACHE:
        _NC_CACHE["nc"] = build_program(DEPTH)
    nc = _NC_CACHE["nc"]
    maps = _in_maps(inputs, list(range(8)))
    res = run_bass_kernel_spmd(nc, maps, core_ids=list(range(8)))
    out = np.stack([np.ascontiguousarray(r["yT"].T) for r in res.results], axis=0)
    return out.astype(np.float32)
```

```python
import math
from contextlib import ExitStack
import numpy as np
import concourse.bass as bass
import concourse.mybir as mybir
from concourse.bass_utils import run_bass_kernel_spmd

F32 = mybir.dt.float32
BF16 = mybir.dt.bfloat16
ALU = mybir.AluOpType
AF = mybir.ActivationFunctionType
AX = mybir.AxisListType

D = 1024
S = 2048
NTT = 4
KC = 8
DFF = 2816
NCP = 22
DEPTH = 4
INC = 6656
OFF_CB, OFF_CC, OFF_CV, OFF_Q, OFF_K, OFF_V, OFF_QM, OFF_GA, OFF_GB, OFF_GC = (
    0, 512, 1024, 1536, 2048, 2560, 3072, 3584, 4608, 5632)
EBW = 1920
NPRM = 252
P_GPRE, P_GPOST, P_GFFN, P_GPFFN, P_GMEM, P_BG, P_CW, P_FW, P_FB = 0, 8, 16, 24, 32, 40, 64, 76, 208
NEGM = -30000.0
EPS = 1e-6
C_ONES, C_ID, C_E, C_AGG, C_END = 0, 128, 256, 1280, 1792
GELU_FUNC = [AF.Gelu_apprx_tanh]
USE_FAST_RECIP = [False]


def _esize(dtp):
    return 4 if dtp == F32 else 2


class Sched:
    ENGS = ("pe", "act", "dve", "pool", "sp")
    NDMA = 8
    GRAN = 64

    def __init__(self, nc, stack):
        self.nc = nc
        self.sem = {e: stack.enter_context(nc.semaphore("s_" + e)) for e in self.ENGS}
        self.dsem = {e: [stack.enter_context(nc.semaphore("d_%s%d" % (e, i)))
                         for i in range(self.NDMA)] for e in ("sp", "pool")}
        self.ndma = {e: 0 for e in ("sp", "pool")}
        self.ops = {e: [] for e in self.ENGS}
        self.seen = {e: {} for e in self.ENGS}
        self.gw = {}
        self.gr = {}
        self.track = {}

    def register(self, name, nbytes):
        n = (nbytes + self.GRAN - 1) // self.GRAN
        self.track[name] = nbytes
        self.gw[name] = [None] * n
        self.gr[name] = [None] * n

    def _range(self, ap):
        name = ap.name
        if name not in self.track:
            return None
        pat = ap.ap
        pstep = pat[0][0]
        es = _esize(ap.dtype)
        off = ap.offset % pstep if pstep else ap.offset
        ext = 1
        for st_, cn in pat[1:]:
            ext += (cn - 1) * abs(st_)
        lo = off * es
        hi = (off + ext) * es
        return name, lo // self.GRAN, (hi + self.GRAN - 1) // self.GRAN

    def _need(self, eng, ev, waits):
        if ev is None:
            return
        if ev[0] == "c":
            _, e, idx = ev
            if e == eng and e == "pe":
                return
            key = ("c", e)
            if self.seen[eng].get(key, -1) >= idx:
                return
            self.seen[eng][key] = idx
            self.ops[e][idx]["sig"] = True
            waits.append(ev)
        else:
            _, q, n = ev
            key = ("d", q, n % self.NDMA)
            if self.seen[eng].get(key, -1) >= n:
                return
            self.seen[eng][key] = n
            waits.append(ev)

    def op(self, eng, fn, r=(), w=(), dma=False):
        waits = []
        idx = len(self.ops[eng])
        rr = []
        ww = []
        for ap in r:
            g = self._range(ap)
            if g is None:
                continue
            (ww if g[0].startswith("ps") else rr).append(g)
        for ap in w:
            g = self._range(ap)
            if g is not None:
                ww.append(g)
        for name, a, b in rr:
            gw = self.gw[name]
            last = None
            for i in range(a, b):
                wv = gw[i]
                if wv is not last:
                    self._need(eng, wv, waits)
                    last = wv
        for name, a, b in ww:
            gw, gr = self.gw[name], self.gr[name]
            last = None
            for i in range(a, b):
                wv = gw[i]
                if wv is not last:
                    self._need(eng, wv, waits)
                    last = wv
                rd = gr[i]
                if rd:
                    for k, v in rd.items():
                        if k == "dma":
                            for dv in v:
                                self._need(eng, dv, waits)
                        elif k != eng:
                            self._need(eng, ("c", k, v), waits)
        rec = {"fn": fn, "waits": waits, "sig": False, "dma": None}
        if dma:
            n = self.ndma[eng]
            self.ndma[eng] += 1
            if n >= self.NDMA:
                self._need(eng, ("d", eng, n - self.NDMA), waits)
            rec["dma"] = n
            ev = ("d", eng, n)
        else:
            ev = ("c", eng, idx)
        self.ops[eng].append(rec)
        for name, a, b in rr:
            gr = self.gr[name]
            for i in range(a, b):
                rd = gr[i]
                if rd is None:
                    rd = gr[i] = {}
                if dma:
                    rd.setdefault("dma", []).append(ev)
                else:
                    rd[eng] = idx
        for name, a, b in ww:
            gw, gr = self.gw[name], self.gr[name]
            for i in range(a, b):
                gw[i] = ev
                gr[i] = None
        return ev

    def emit(self, block, final_events=()):
        fw = []
        for ev in final_events:
            self._need("sp", ev, fw)
        self.ops["sp"].append({"fn": None, "waits": fw, "sig": False, "dma": None})
        cnt = {}
        for e in self.ENGS:
            c = 0
            lst = []
            for rec in self.ops[e]:
                if rec["sig"]:
                    c += 1
                lst.append(c)
            cnt[e] = lst
        hsem, dsem, ND = self.sem, self.dsem, self.NDMA

        def run(e, h):
            for rec in self.ops[e]:
                for ev in rec["waits"]:
                    if ev[0] == "c":
                        h.wait_ge(hsem[ev[1]], cnt[ev[1]][ev[2]])
                    else:
                        h.wait_ge(dsem[ev[1]][ev[2] % ND], 16 * (ev[2] // ND + 1))
                if rec["fn"] is None:
                    continue
                ins = rec["fn"](h)
                if rec["dma"] is not None:
                    ins.then_inc(dsem[e][rec["dma"] % ND], 16)
                elif rec["sig"]:
                    ins.then_inc(hsem[e], 1)

        @block.tensor
        def _(h):
            run("pe", h)

        @block.scalar
        def _(h):
            run("act", h)

        @block.vector
        def _(h):
            run("dve", h)

        @block.gpsimd
        def _(h):
            run("pool", h)

        @block.sync
        def _(h):
            run("sp", h)


def build_program(NL=DEPTH, dbg=None, stop_after=None):
    nc = bass.Bass("TRN2", target_bir_lowering=False)
    dt = nc.dram_tensor
    xT_d = dt("xT", [D, S], F32, kind="ExternalInput").ap()
    memT_d = dt("memT", [D, 256], F32, kind="ExternalInput").ap()
    prm_d = dt("prm", [128, DEPTH * NPRM], F32, kind="ExternalInput").ap()
    tbl_d = dt("tbl", [8, 128, EBW], F32, kind="ExternalInput").ap()
    b31_d = dt("b31", [128, 8], F32, kind="ExternalInput").ap()
    cst_d = dt("cst", [128, C_END], F32, kind="ExternalInput").ap()
    w_in_d = dt("w_in", [DEPTH, D, INC], F32, kind="ExternalInput").ap()
    w_co_d = dt("w_conv_out", [DEPTH, 512, D], F32, kind="ExternalInput").ap()
    w_ao_d = dt("w_attn_out", [DEPTH, 512, D], F32, kind="ExternalInput").ap()
    w_kv_d = dt("w_mem_kv", [DEPTH, D, 1024], F32, kind="ExternalInput").ap()
    w_mo_d = dt("w_mem_out", [DEPTH, 512, D], F32, kind="ExternalInput").ap()
    w_out_d = dt("w_out", [DEPTH, D, D], F32, kind="ExternalInput").ap()
    w_up_d = dt("w_up", [DEPTH, D, 2 * DFF], F32, kind="ExternalInput").ap()
    w_dn_d = dt("w_down", [DEPTH, DFF, D], F32, kind="ExternalInput").ap()
    yT_d = dt("yT", [D, S], F32, kind="ExternalOutput").ap()
    ebx_d = dt("ebx", [8, 128, EBW], BF16, kind="Internal").ap()
    dbg_d = {}
    if dbg:
        for name, shape in dbg.items():
            dbg_d[name] = dt("dbg_" + name, list(shape), F32, kind="ExternalOutput").ap()

    with ExitStack() as st:
        sc = Sched(nc, st)

        def sb(name, shape, dtp):
            t = st.enter_context(nc.sbuf_tensor(name, shape, dtp))
            n = 1
            for s_ in shape[1:]:
                n *= s_
            sc.register(name, n * _esize(dtp))
            return t

        xT = sb("xTs", [128, KC, S], F32)
        hTt = sb("hTs", [128, KC * S], BF16)
        Rt = sb("Rs", [128, 32768], BF16)
        EBt = sb("EBs", [128, 2, EBW], BF16)
        wbt = sb("wbs", [128, 8448], BF16)
        ARt = sb("ARs", [128, 5632], BF16)
        prm = sb("prms", [128, DEPTH * NPRM], F32)
        cst = sb("csts", [128, C_END], BF16)
        b31 = sb("b31s", [128, 8], F32)
        epsb = sb("epsb", [128, 1], F32)
        ksum = sb("ksums", [128, 4, 8], F32)
        dK32 = sb("dK32s", [128, 4, 64], F32)
        dKh = sb("dKhs", [128, 4, 128], BF16)
        dKl = sb("dKls", [128, 4, 128], BF16)
        hal = sb("hals", [128, 2 * NCP, 2], BF16)
        ps = []
        for i in range(8):
            ps.append(st.enter_context(nc.psum_tensor("ps%d" % i, [128, 512], F32)))
            sc.register("ps%d" % i, 2048)
        block = st.enter_context(nc.Block())

        hT = hTt[:, :].rearrange("p (c t) -> p c t", c=KC)
        R = Rt[:, :]
        AR = ARt[:, :]
        ones = cst[:, C_ONES:C_ONES + 128]
        ident = cst[:, C_ID:C_ID + 128]
        final_events = []

        def v3(ap2, c):
            return ap2.rearrange("p (c t) -> p c t", c=c)

        def isap(x):
            return not isinstance(x, (int, float)) and x is not None

        def MM(out, lhsT, rhs, start, stop, skip=False):
            if skip:
                sc.op("pe", lambda h: h.matmul(out, lhsT, rhs, start=start, stop=stop, skip_group_check=True),
                      r=[lhsT, rhs], w=[out])
            else:
                sc.op("pe", lambda h: h.matmul(out, lhsT, rhs, start=start, stop=stop), r=[lhsT, rhs], w=[out])

        def ACT(out, in_, func, bias=None, scale=1.0):
            r = [in_] + ([bias] if isap(bias) else [])
            if bias is None:
                sc.op("act", lambda h: h.activation(out=out, in_=in_, func=func, scale=scale), r=r, w=[out])
            else:
                sc.op("act", lambda h: h.activation(out=out, in_=in_, func=func, bias=bias, scale=scale), r=r, w=[out])

        def TT(eng, out, in0, in1, op):
            sc.op(eng, lambda h: h.tensor_tensor(out=out, in0=in0, in1=in1, op=op), r=[in0, in1], w=[out])

        def TS(eng, out, in0, s1, s2, op0, op1=None):
            r = [in0] + [x for x in (s1, s2) if isap(x)]
            if op1 is None:
                sc.op(eng, lambda h: h.tensor_scalar(out=out, in0=in0, scalar1=s1, scalar2=None, op0=op0), r=r, w=[out])
            else:
                sc.op(eng, lambda h: h.tensor_scalar(out=out, in0=in0, scalar1=s1, scalar2=s2, op0=op0, op1=op1),
                      r=r, w=[out])

        def STT(out, in0, scalar, in1, op0, op1):
            r = [in0, in1] + ([scalar] if isap(scalar) else [])
            sc.op("dve", lambda h: h.scalar_tensor_tensor(out=out, in0=in0, scalar=scalar, in1=in1, op0=op0, op1=op1),
                  r=r, w=[out])

        def COPY(eng, out, in_):
            if eng == "act":
                ACT(out, in_, AF.Copy)
            else:
                sc.op(eng, lambda h: h.tensor_copy(out=out, in_=in_), r=[in_], w=[out])

        def RECIP(out, in_):
            sc.op("dve", lambda h: h.reciprocal(out=out, in_=in_), r=[in_], w=[out])

        def RECIPF(out, in_):
            if USE_FAST_RECIP[0]:
                sc.op("dve", lambda h: h.reciprocal_approx_fast(out=out, in_=in_), r=[in_], w=[out])
            else:
                RECIP(out, in_)

        def MEMSET(eng, out, val):
            sc.op(eng, lambda h: h.memset(out, val), w=[out])

        def DMA(q, out, in_):
            return sc.op(q, lambda h: h.dma_start(out=out, in_=in_), r=[in_], w=[out], dma=True)

        wstate = {"n": 0, "n3": 0}

        def wload(parts, three=False):
            if three:
                base = (wstate["n3"] % 3) * 2816
                wstate["n3"] += 1
            else:
                base = (wstate["n"] % 2) * 4096
                wstate["n"] += 1
            for (off, kc, ncols, src) in parts:
                DMA("pool", wbt[:, base + off:base + off + kc * ncols].rearrange("p (k n) -> p k n", k=kc), src)
            return base

        def wview(base, off, kc, ncols):
            return wbt[:, base + off:base + off + kc * ncols].rearrange("p (k n) -> p k n", k=kc)

        def wsrc(w_d, l, nk, c0, ncols):
            return w_d[l, 0:nk * 128, c0:c0 + ncols].rearrange("(k p) n -> p k n", p=128)

        steps = []

        preissued = []

        def run_steps(depth=1, next_ld=None):
            n = len(steps)
            slots = [None] * n
            issued = 0
            if preissued and n and steps[0][0] is not None:
                slots[0] = preissued.pop(0)
                issued = 1

            def issue_upto(k):
                nonlocal issued
                while issued < min(k + 1, n):
                    if steps[issued][0] is not None:
                        slots[issued] = steps[issued][0]()
                    issued += 1
            for i, (ld, comp) in enumerate(steps):
                issue_upto(i + depth)
                if i == n - 1 and next_ld is not None:
                    preissued.append(next_ld())
                comp(slots[i])
            steps.clear()

        rot = {"n": 0}

        def next_ps(lo=0, n=8):
            i = lo + rot["n"] % n
            rot["n"] += 1
            return i

        def pcol(l, off):
            return prm[:, l * NPRM + off:l * NPRM + off + 1]

        def ar_bf(off, n):
            return AR[:, off:off + n]

        def ar_f32(off, n):
            return AR[:, off:off + n].bitcast(F32)

        def tsl(tt):
            return slice(tt * 512, (tt + 1) * 512)

        sc.register("ebx", EBW * 2)
        for h_ in range(8):
            DMA("pool", EBt[:, h_ % 2, :], tbl_d[h_])
            ACT(EBt[:, h_ % 2, :], EBt[:, h_ % 2, :], AF.Exp)
            DMA("sp", ebx_d[h_], EBt[:, h_ % 2, :])
        DMA("sp", prm[:, :], prm_d)
        DMA("sp", b31[:, :], b31_d)
        DMA("pool", cst[:, :], cst_d)
        MEMSET("dve", epsb[:, :], EPS)
        for c in range(KC):
            DMA("sp", xT[:, c, :], xT_d[c * 128:(c + 1) * 128, :])

        def rstd_from_ps(psi, ncol, rs_ap, rstd_ap):
            ACT(rs_ap, ps[psi][:, 0:ncol], AF.Ln, bias=epsb[:, 0:1], scale=1.0 / D)
            ACT(rstd_ap, rs_ap, AF.Exp, scale=-0.5)

        nk_ = {"n": 0}
        EBf = EBt[:, :, :].rearrange("p a b -> p (a b)")
        tmpA = EBf[:, 0:1024].bitcast(F32)
        rstdR = [EBf[:, 1024:2048].bitcast(F32), EBf[:, 2048:3072].bitcast(F32)]
        NSQ = [ar_bf(3584 + 512 * i, 512) for i in range(4)]
        NRSTD = ar_f32(4608, 1024)

        def rstd_ps(bank, rstd):
            ACT(rstd, ps[bank][:, :], AF.Ln, bias=epsb[:, 0:1], scale=1.0 / D)
            ACT(rstd, rstd, AF.Exp, scale=-0.5)

        def norm_tt(l, goff, tt):
            psi = next_ps(0, 4)
            for c in range(KC):
                k = nk_["n"]
                nk_["n"] += 1
                ACT(NSQ[k % 4], xT[:, c, tsl(tt)], AF.Square)
                MM(ps[psi][:, :], ones, NSQ[k % 4], c == 0, c == KC - 1)
            rstd_ps(psi, NRSTD)
            for c in range(KC):
                STT(hT[:, c, tsl(tt)], xT[:, c, tsl(tt)], pcol(l, goff + c), NRSTD, ALU.mult, ALU.mult)

        def resid_apply(l, goff, tt, ov, rstd, tmps):
            for c in range(KC):
                k = nk_["n"]
                nk_["n"] += 1
                tmp = tmps[k % len(tmps)]
                STT(tmp, ov(c), pcol(l, goff + c), rstd, ALU.mult, ALU.mult)
                TT("pool", xT[:, c, tsl(tt)], xT[:, c, tsl(tt)], tmp, ALU.add)

        dq = []

        def make_pieces(l_res, goff_res, tt, ov, rstd, l_norm, goff_norm):
            P = []
            for c in range(KC):
                def rp(c=c):
                    STT(tmpA, ov(c), pcol(l_res, goff_res + c), rstd, ALU.mult, ALU.mult)
                    TT("pool", xT[:, c, tsl(tt)], xT[:, c, tsl(tt)], tmpA, ALU.add)
                P.append(rp)
            if l_norm is not None:
                def stats():
                    psi = next_ps(0, 4)
                    for c in range(KC):
                        k = nk_["n"]
                        nk_["n"] += 1
                        ACT(NSQ[k % 4], xT[:, c, tsl(tt)], AF.Square)
                        MM(ps[psi][:, :], ones, NSQ[k % 4], c == 0, c == KC - 1)
                    rstd_ps(psi, NRSTD)
                P.append(stats)
                for c in range(KC):
                    P.append(lambda c=c: STT(hT[:, c, tsl(tt)], xT[:, c, tsl(tt)], pcol(l_norm, goff_norm + c),
                                             NRSTD, ALU.mult, ALU.mult))
            return P

        def drain(n):
            for _ in range(min(n, len(dq))):
                dq.pop(0)()

        Vv = R[:, 0:8192].rearrange("p (i c) -> p i c", i=16)
        QT = v3(R[:, 8192:16384], 4)
        KT = v3(R[:, 16384:24576], 4)
        Vh = [R[:, 24576 + s * 2048:24576 + (s + 1) * 2048].rearrange("p (i c) -> p i c", i=16)
              for s in range(2)]
        QTm = [R[:, 28672 + s * 2048:28672 + (s + 1) * 2048] for s in range(2)]
        early = {}

        def ld_cols_l(l, c0):
            return lambda: wload([(0, KC, 512, wsrc(w_in_d, l, KC, c0, 512))])

        def proj_T(s, dstT, tts=(0, 1, 2, 3)):
            W = wview(s, 0, KC, 512)
            for j in range(4):
                for tt in tts:
                    psi = next_ps()
                    for kc in range(KC):
                        MM(ps[psi][:, :], W[:, kc, j * 128:(j + 1) * 128], hT[:, kc, tsl(tt)],
                           kc == 0, kc == KC - 1)
                    COPY("act", dstT[:, j, tsl(tt)], ps[psi][:, :])

        def k_stats():
            for j in range(4):
                sc.op("dve", lambda h, j=j: h.tensor_reduce(
                    out=ksum[:, j, :], in_=KT[:, j, :].rearrange("p (n k) -> p n k", n=8),
                    axis=AX.X, op=ALU.add), r=[KT[:, j, :]], w=[ksum[:, j, :]])
                TT("dve", dK32[:, j, :].rearrange("p (n m) -> p n m", n=8),
                   ksum[:, j, :].unsqueeze(1).broadcast_to([128, 8, 8]),
                   ksum[:, j, :].unsqueeze(2).broadcast_to([128, 8, 8]), ALU.subtract)
                COPY("dve", dKh[:, j, 0:64], dK32[:, j, :])
                TT("dve", dKl[:, j, 0:64], dK32[:, j, :], dKh[:, j, 0:64], ALU.subtract)
                COPY("pool", dKh[:, j, 64:128], dKh[:, j, 0:64])
                COPY("pool", dKl[:, j, 64:128], dKl[:, j, 0:64])

        def v_proj(s, irange):
            W = wview(s, 0, KC, 512)
            for i in irange:
                psi = next_ps()
                for kc in range(KC):
                    MM(ps[psi][:, :], hT[:, kc, i * 128:(i + 1) * 128], W[:, kc, :], kc == 0, kc == KC - 1)
                COPY("act", Vv[:, i, :], ps[psi][:, :])

        def early_proj_steps(l2):
            def ka(s):
                early["K"] = s
                proj_T(s, KT, (0, 1))

            def va(s):
                early["V"] = s
                v_proj(s, range(8))
            return [(ld_cols_l(l2, OFF_K), ka), (ld_cols_l(l2, OFF_V), va)]

        def dump(name, view_fn, nchunk, ncol):
            tmpo = ar_f32(4096, 1024)
            for c in range(nchunk):
                for t0 in range(0, ncol, 512):
                    n = min(512, ncol - t0)
                    COPY("dve", tmpo[:, 0:n], view_fn(c, t0, n))
                    final_events.append(DMA("sp", dbg_d[name][c * 128:(c + 1) * 128, t0:t0 + n], tmpo[:, 0:n]))

        for l in range(NL):
            last = (l == NL - 1)
            if l == 0:
                for tt in range(NTT):
                    norm_tt(0, P_GPRE, tt)
            if dbg and "hT" in dbg and last:
                dump("hT", lambda c, t0, n: hT[:, c, t0:t0 + n], KC, S)

            def ld_cols(c0):
                return ld_cols_l(l, c0)

            def gate_ld(gate_off, w_bo_d, q4):
                return lambda: wload([(0, KC, 256, wsrc(w_in_d, l, KC, gate_off + q4 * 256, 256)),
                                      (2048, 4, 256, wsrc(w_bo_d, l, 4, q4 * 256, 256))])

            def ld_wo(g2):
                return lambda: wload([(0, KC, 512, wsrc(w_out_d, l, KC, g2 * 512, 512))])

            def ld_up(cp0):
                return lambda: wload([(0, KC, 256, wsrc(w_up_d, l, KC, cp0 * 128, 256)),
                                      (2048, KC, 256, wsrc(w_up_d, l, KC, DFF + cp0 * 128, 256))])

            if l == 0:
                def comp_K(s):
                    proj_T(s, KT)
                    k_stats()
                steps.append((ld_cols(OFF_K), comp_K))
                steps.append((ld_cols(OFF_V), lambda s: v_proj(s, range(16))))
            else:
                def comp_Kb(_s):
                    proj_T(early["K"], KT, (2, 3))
                    k_stats()
                steps.append((None, comp_Kb))
                steps.append((None, lambda _s: v_proj(early["V"], range(8, 16))))
            steps.append((ld_cols(OFF_Q), lambda s: proj_T(s, QT)))
            run_steps(next_ld=None if stop_after in ("proj", "att") else gate_ld(OFF_GB, w_ao_d, 0))
            if dbg and "QT" in dbg and last:
                dump("QT", lambda c, t0, n: QT[:, c, t0:t0 + n], 4, S)
            if dbg and "KT" in dbg and last:
                dump("KT", lambda c, t0, n: KT[:, c, t0:t0 + n], 4, S)
            if stop_after == "proj":
                break

            LA, NSB, NPB = 4, 5, 6
            Pt = [ar_bf(i * 512, 512) for i in range(NPB)]
            ind = ar_bf(3072, 512)
            negs = [ar_bf(3584, 512), ar_bf(4096, 512)]
            rec = ar_f32(4608, 1024)

            def head_prep(h_):
                e = h_ % 2
                jc, hp = h_ // 2, (h_ % 2) * 64
                DMA("sp", EBt[:, e, :], ebx_d[h_])
                voff, ooff = (0, 64) if e == 0 else (64, 0)
                zp = 64 - hp
                if h_ < 2:
                    MEMSET("pool", Vh[e][:, :, ooff:ooff + 64], 1.0)
                    MEMSET("pool", QTm[e][zp:zp + 64, :], 0.0)
                COPY("dve", Vh[e][:, :, voff:voff + 64], Vv[:, :, h_ * 64:(h_ + 1) * 64])
                COPY("dve", QTm[e][hp:hp + 64, :], QT[hp:hp + 64, jc, :])

            def sel_prep1(h_, j):
                e, jc = h_ % 2, h_ // 2
                qsl = QTm[e][:, j * 512:(j + 1) * 512]
                MM(ps[0][:, :], dKh[:, jc, :], qsl, True, False)
                MM(ps[0][:, :], dKl[:, jc, :], qsl, False, True)
                TS("dve", ind, ps[0][:, :], 0.0, None, ALU.is_gt)

            def sel_prep2(h_, j):
                ng = negs[j % 2]
                for hf in range(2):
                    qb = 2 * j + hf
                    MM(ps[0][:, hf * 256:(hf + 1) * 256],
                       cst[:, C_AGG + (qb - 4) * 128:C_AGG + (qb - 3) * 128],
                       ind[:, hf * 256:(hf + 1) * 256], True, True, skip=True)
                TS("dve", ng, ps[0][:, :], 2.5, NEGM, ALU.is_ge, ALU.mult)

            tiles = [(h_, j, kt) for h_ in range(8) for j in range(NTT) for kt in range(4 * j + 4)]
            NT_ = len(tiles)

            def stageA(t):
                h_, j, kt = tiles[t]
                e, jc = h_ % 2, h_ // 2
                if (j, kt) == (0, 0):
                    sel_prep1(h_, 2)
                elif (j, kt) == (1, 2):
                    sel_prep2(h_, 2)
                elif (j, kt) == (1, 6):
                    sel_prep1(h_, 3)
                elif (j, kt) == (2, 4):
                    sel_prep2(h_, 3)
                q0, k0, n = j * 512, kt * 128, kt // 2
                qsl = QTm[e][:, q0:q0 + 512]
                pS = 1 + t % NSB
                use_mask = j >= 2 and n <= 2 * j
                c0 = 256 if kt >= 4 * j + 2 else 0
                MM(ps[pS][:, c0:512], KT[:, jc, k0:k0 + 128], qsl[:, c0:512], True, not use_mask)
                if use_mask:
                    MM(ps[pS][:, :], cst[:, C_E + n * 128:C_E + (n + 1) * 128], negs[j % 2], False, True)
                pt = Pt[t % NPB]
                delta = q0 - k0
                if delta >= 1024:
                    ACT(pt, ps[pS][:, :], AF.Exp, bias=b31[:, h_:h_ + 1], scale=0.125)
                else:
                    ACT(pt[:, c0:512], ps[pS][:, c0:512], AF.Exp, scale=0.125)
                    u0 = delta + 512
                    TT("dve" if t % 2 == 0 else "pool", pt[:, c0:512], pt[:, c0:512],
                       EBt[:, e, u0 + c0:u0 + 512], ALU.mult)

            def stageC(t):
                h_, j, kt = tiles[t]
                e, jc = h_ % 2, h_ // 2
                nk = 4 * j + 4
                pO = 6 + (h_ * NTT + j) % 2
                c0 = 256 if kt >= 4 * j + 2 else 0
                MM(ps[pO][:, c0:512], Vh[e][:, kt, :], Pt[t % NPB][:, c0:512], kt == 0, kt == nk - 1,
                   skip=(c0 != 0))
                if kt == nk - 1:
                    vr = slice(0, 64) if e == 0 else slice(64, 128)
                    sr = slice(64, 128) if e == 0 else slice(0, 64)
                    ACT(rec[vr, :], ps[pO][sr, :], AF.Ln)
                    ACT(rec[vr, :], rec[vr, :], AF.Exp, scale=-1.0)
                    TT("dve", QT[vr, jc, j * 512:(j + 1) * 512], ps[pO][vr, :], rec[vr, :], ALU.mult)
                    if j == NTT - 1 and h_ + 2 < 8:
                        head_prep(h_ + 2)

            head_prep(0)
            head_prep(1)
            for t in range(NT_ + LA):
                if t < NT_:
                    stageA(t)
                if t >= LA:
                    stageC(t - LA)
            OT = QT
            if dbg and "OT" in dbg and last:
                dump("OT", lambda c, t0, n: OT[:, c, t0:t0 + n], 4, S)
            if stop_after == "att":
                break

            mrg = v3(R[:, 16384:32768], KC)
            sg = [ar_bf(0, 512), ar_bf(512, 512)]
            gtmp = [ar_bf(1024, 512), ar_bf(1536, 512)]
            gk = {"n": 0}

            def gate_branch(bidx, gate_off, w_bo_d, srcT, first):
                for q4 in range(4):
                    ld = gate_ld(gate_off, w_bo_d, q4)

                    def comp(s, q4=q4):
                        Wg = wview(s, 0, KC, 256)
                        Wb = wview(s, 2048, 4, 256)
                        for o2 in range(2):
                            oc = q4 * 2 + o2
                            for tt in range(NTT):
                                pg = next_ps()
                                for kc in range(KC):
                                    MM(ps[pg][:, :], Wg[:, kc, o2 * 128:(o2 + 1) * 128], hT[:, kc, tsl(tt)],
                                       kc == 0, kc == KC - 1)
                                k = gk["n"]
                                gk["n"] += 1
                                ACT(sg[k % 2], ps[pg][:, :], AF.Sigmoid, bias=pcol(l, P_BG + bidx * 8 + oc))
                                py = next_ps()
                                for kc in range(4):
                                    MM(ps[py][:, :], Wb[:, kc, o2 * 128:(o2 + 1) * 128], srcT[:, kc, tsl(tt)],
                                       kc == 0, kc == 3)
                                dst = mrg[:, oc, tsl(tt)]
                                if first:
                                    TT("dve", dst, ps[py][:, :], sg[k % 2], ALU.mult)
                                else:
                                    TT("dve", gtmp[k % 2], ps[py][:, :], sg[k % 2], ALU.mult)
                                    TT("pool", dst, dst, gtmp[k % 2], ALU.add)
                    steps.append((ld, comp))

            gate_branch(1, OFF_GB, w_ao_d, OT, True)
            run_steps(next_ld=None if stop_after == "brB" else ld_cols(OFF_CC))
            if stop_after == "brB":
                if dbg and "mrg" in dbg and last:
                    dump("mrg", lambda c, t0, n: mrg[:, c, t0:t0 + n], KC, S)
                break

            ccT = v3(R[:, 0:8192], 4)
            zT = v3(R[:, 8192:16384], 4)
            dgm = ar_bf(2048, 1536).rearrange("p (a b) -> p a b", a=12)
            for j in range(4):
                for tap in range(3):
                    TS("dve", dgm[:, j * 3 + tap, :], ident, pcol(l, P_CW + tap * 4 + j), None, ALU.mult)

            def comp_cc(s):
                proj_T(s, ccT)

            def comp_cv(s):
                W = wview(s, 0, KC, 512)
                for j in range(4):
                    for tt in range(NTT):
                        psi = next_ps()
                        for kc in range(KC):
                            MM(ps[psi][:, :], W[:, kc, j * 128:(j + 1) * 128], hT[:, kc, tsl(tt)],
                               kc == 0, kc == KC - 1)
                        TT("dve", zT[:, j, tsl(tt)], ps[psi][:, :], ccT[:, j, tsl(tt)], ALU.mult)

            def comp_cb(s):
                W = wview(s, 0, KC, 512)
                for j in range(4):
                    for tt in range(NTT):
                        pc = next_ps()
                        t0 = tt * 512
                        for tap in range(3):
                            sh = 2 - tap
                            lhs = dgm[:, j * 3 + tap, :]
                            if sh == 0:
                                MM(ps[pc][:, :], lhs, zT[:, j, t0:t0 + 512], tap == 0, True, skip=True)
                            elif tt == 0:
                                MM(ps[pc][:, sh:512], lhs, zT[:, j, 0:512 - sh], tap == 0, False, skip=True)
                            else:
                                MM(ps[pc][:, :], lhs, zT[:, j, t0 - sh:t0 - sh + 512], tap == 0, False, skip=True)
                        COPY("act", ccT[:, j, tsl(tt)], ps[pc][:, :])
                        psi = next_ps()
                        for kc in range(KC):
                            MM(ps[psi][:, :], W[:, kc, j * 128:(j + 1) * 128], hT[:, kc, tsl(tt)],
                               kc == 0, kc == KC - 1)
                        TT("dve", ccT[:, j, tsl(tt)], ps[psi][:, :], ccT[:, j, tsl(tt)], ALU.mult)

            qmT = v3(R[:, 0:8192], 4)
            Y = R[:, 8192:16384]
            memf = Y[:, 0:4096].bitcast(F32).rearrange("p (c m) -> p c m", c=KC)
            memn = Y[:, 4096:6144].rearrange("p (c m) -> p c m", c=KC)
            mkT = Y[:, 6144:7168].rearrange("p (c m) -> p c m", c=4)
            mv = Y[:, 7168:8192].rearrange("p (i c) -> p i c", i=2)
            msq8 = [ar_bf(3584 + 256 * c, 256) for c in range(KC)]
            mrs = EBf[:, 3072:3584].bitcast(F32)
            mrstd = EBf[:, 0:512].bitcast(F32)

            def mem_prep(_s):
                DMA("sp", memf, memT_d.rearrange("(c p) m -> p c m", p=128))
                for c in range(KC):
                    ACT(msq8[c], memf[:, c, :], AF.Square)

            def mem_prep2(_s):
                pm = next_ps()
                for c in range(KC):
                    MM(ps[pm][:, 0:256], ones, msq8[c], c == 0, c == KC - 1)
                rstd_from_ps(pm, 256, mrs, mrstd)
                for c in range(KC):
                    STT(memn[:, c, :], memf[:, c, :], pcol(l, P_GMEM + c), mrstd, ALU.mult, ALU.mult)

            steps.append((ld_cols(OFF_CC), comp_cc))
            steps.append((ld_cols(OFF_CV), comp_cv))
            steps.append((ld_cols(OFF_CB), comp_cb))
            steps.append((None, mem_prep))
            gate_branch(0, OFF_GA, w_co_d, ccT, False)
            steps.insert(len(steps) - 3, (None, mem_prep2))
            run_steps(next_ld=None if stop_after == "brA" else ld_cols(OFF_QM))
            if stop_after == "brA":
                if dbg and "mrg" in dbg and last:
                    dump("mrg", lambda c, t0, n: mrg[:, c, t0:t0 + n], KC, S)
                break

            def ld_kv(c0):
                return lambda: wload([(0, KC, 512, wsrc(w_kv_d, l, KC, c0, 512))])

            def comp_mk(s):
                W = wview(s, 0, KC, 512)
                for j in range(4):
                    psi = next_ps()
                    for kc in range(KC):
                        MM(ps[psi][:, 0:256], W[:, kc, j * 128:(j + 1) * 128], memn[:, kc, :],
                           kc == 0, kc == KC - 1)
                    COPY("act", mkT[:, j, :], ps[psi][:, 0:256])

            def comp_mv(s):
                W = wview(s, 0, KC, 512)
                for i in range(2):
                    psi = next_ps()
                    for kc in range(KC):
                        MM(ps[psi][:, :], memn[:, kc, i * 128:(i + 1) * 128], W[:, kc, :], kc == 0, kc == KC - 1)
                    COPY("act", mv[:, i, :], ps[psi][:, :])

            def comp_qm(s):
                proj_T(s, qmT)

            def mem_attn():
                NMP = 4
                mP = [ar_bf(i * 512, 512) for i in range(NMP)]
                mrec = ar_f32(3072, 1024)
                mt = [(hm, tt, i) for hm in range(4) for tt in range(NTT) for i in range(2)]

                def mA(t):
                    hm, tt, i = mt[t]
                    pS = 4 + t % 4
                    MM(ps[pS][:, :], mkT[:, hm, i * 128:(i + 1) * 128], qmT[:, hm, tsl(tt)], True, True)
                    ACT(mP[t % NMP], ps[pS][:, :], AF.Exp, scale=128.0 ** -0.5)

                def mC(t):
                    hm, tt, i = mt[t]
                    g = t // 2
                    pC, pSm = (g % 2) * 2, (g % 2) * 2 + 1
                    MM(ps[pC][:, :], mv[:, i, hm * 128:(hm + 1) * 128], mP[t % NMP], i == 0, i == 1)
                    MM(ps[pSm][:, :], ones, mP[t % NMP], i == 0, i == 1)
                    if i == 1:
                        ACT(mrec, ps[pSm][:, :], AF.Ln)
                        ACT(mrec, mrec, AF.Exp, scale=-1.0)
                        TT("dve", qmT[:, hm, tsl(tt)], ps[pC][:, :], mrec, ALU.mult)

                for t in range(len(mt) + 2):
                    if t < len(mt):
                        mA(t)
                    if t >= 2:
                        mC(t - 2)

            def comp_mv_att(s):
                comp_mv(s)
                mem_attn()

            steps.append((ld_cols(OFF_QM), comp_qm))
            steps.append((ld_kv(0), comp_mk))
            steps.append((ld_kv(512), comp_mv_att))
            gate_branch(2, OFF_GC, w_mo_d, qmT, False)
            run_steps(next_ld=None if stop_after == "brC" else ld_wo(0))
            if dbg and "mrg" in dbg and last:
                dump("mrg", lambda c, t0, n: mrg[:, c, t0:t0 + n], KC, S)
            if stop_after == "brC":
                break

            oT4 = R[:, 0:16384].rearrange("p (t c n) -> p t c n", t=4, c=KC)
            fdg = [ar_bf(s_ * 768, 768).rearrange("p (a b) -> p a b", a=6) for s_ in range(2)]
            gel = [ar_bf(1536, 512), ar_bf(2048, 512)]
            osq = [ar_bf(2560, 512), ar_bf(3072, 512)]
            rstdI = [ar_f32(1536, 1024), ar_f32(2560, 1024)]
            tmpB = ar_f32(0, 1024)
            ok = {"n": 0}
            sspend = []

            wo_slots = {}

            def comp_wo_all(s1):
                Ws = (wview(wo_slots["a"], 0, KC, 512), wview(s1, 0, KC, 512))
                rdst = [rstdI[0], rstdI[1], rstdR[0], rstdR[1]]

                def resid01(tt):
                    resid_apply(l, P_GPOST, tt, lambda c, tt=tt: oT4[:, tt, c, :], rstdI[tt], [tmpA, tmpB])
                for tt in range(NTT):
                    for oc in range(KC):
                        if tt == NTT - 1 and oc == 4 and stop_after != "mix":
                            preissued.append(ld_up(0)())
                        psi = next_ps(0, 4)
                        W = Ws[oc // 4]
                        o4 = oc % 4
                        for kc in range(KC):
                            MM(ps[psi][:, :], W[:, kc, o4 * 128:(o4 + 1) * 128], mrg[:, kc, tsl(tt)],
                               kc == 0, kc == KC - 1)
                        if sspend:
                            sspend.pop()()
                        if oc == 1 and tt >= 1:
                            rstd_ps(4 + tt - 1, rdst[tt - 1])
                            if tt == 1:
                                resid01(0)
                            elif tt == 2:
                                norm_tt(l, P_GFFN, 0)
                                resid01(1)
                            elif tt == 3:
                                norm_tt(l, P_GFFN, 1)
                        sq = osq[ok["n"] % 2]
                        ok["n"] += 1
                        COPY("act", oT4[:, tt, oc, :], ps[psi][:, :])
                        ACT(sq, ps[psi][:, :], AF.Square)
                        sspend.append(lambda sq=sq, tt=tt, oc=oc: MM(ps[4 + tt][:, :], ones, sq, oc == 0, oc == KC - 1))
                sspend.pop()()
                rstd_ps(7, rdst[3])

            steps.append((ld_wo(0), lambda s0: wo_slots.__setitem__("a", s0)))
            steps.append((ld_wo(1), comp_wo_all))
            run_steps()

            def deferred_b1(_s):
                for tt in (2, 3):
                    resid_apply(l, P_GPOST, tt, lambda c, tt=tt: oT4[:, tt, c, :], rstdR[tt - 2], [tmpA])
                    norm_tt(l, P_GFFN, tt)

            def deferred_b2(_s):
                for tt in (0, 1):
                    resid_apply(l, P_GPFFN, tt, lambda c, tt=tt: hT[:, c, tsl(tt)], rstdR[tt], [tmpA])
                    if l + 1 < NL:
                        norm_tt(l + 1, P_GPRE, tt)
            if stop_after == "mix":
                deferred_b1(None)
                break

            for hf in range(2):
                actT = R[:, 0:NCP * 1024].rearrange("p (c t) -> p c t", c=NCP)
                pre = R[:, 22528:22528 + 4104].rearrange("p (a b t) -> p a b t", a=2, b=2)
                fk = {"n": 0}

                pend = []

                def conv_stage(cp, k, sl):
                    def f():
                        for i in range(2):
                            pa, pg = next_ps(4, 4), next_ps(4, 4)
                            for ag, pp in ((0, pa), (1, pg)):
                                for tap in range(3):
                                    MM(ps[pp][:, :], fdg[sl][:, ag * 3 + tap, :],
                                       pre[:, sl, ag, i * 512 + tap:i * 512 + tap + 512], tap == 0, tap == 2)
                            ga = gel[(2 * k + i) % 2]
                            ACT(ga, ps[pa][:, :], GELU_FUNC[0], bias=pcol(l, P_FB + cp))
                            STT(actT[:, cp, tsl(i)], ps[pg][:, :], pcol(l, P_FB + NCP + cp), ga, ALU.add, ALU.mult)
                    return f

                def comp_up(cp0):
                    def f(s):
                        Ws = (wview(s, 0, KC, 256), wview(s, 2048, KC, 256))
                        for c2 in range(2):
                            cp = cp0 + c2
                            k = fk["n"]
                            fk["n"] += 1
                            sl = k % 2
                            for ag in range(2):
                                for tap in range(3):
                                    col = P_FW + tap * 44 + ag * NCP + cp
                                    TS("dve", fdg[sl][:, ag * 3 + tap, :], ident, pcol(l, col), None, ALU.mult)
                            for ag in range(2):
                                if hf == 0:
                                    MEMSET("dve", pre[:, sl, ag, 0:2], 0.0)
                                else:
                                    COPY("dve", pre[:, sl, ag, 0:2], hal[:, ag * NCP + cp, :])
                            for ag in range(2):
                                for i in range(2):
                                    psi = next_ps(0, 4)
                                    for kc in range(KC):
                                        MM(ps[psi][:, :], Ws[ag][:, kc, c2 * 128:(c2 + 1) * 128], hT[:, kc, tsl(2 * hf + i)],
                                           kc == 0, kc == KC - 1)
                                    COPY("act" if ag == 0 else "dve",
                                         pre[:, sl, ag, 2 + i * 512:2 + (i + 1) * 512], ps[psi][:, :])
                            if hf == 0:
                                for ag in range(2):
                                    COPY("dve", hal[:, ag * NCP + cp, :], pre[:, sl, ag, 1024:1026])
                            drain(5 if hf == 0 else 2)
                            if pend:
                                pend.pop()()
                            pend.append(conv_stage(cp, k, sl))
                    return f

                def fill_b1(_s):
                    for tt in (2, 3):
                        dq.extend(make_pieces(l, P_GPOST, tt, lambda c, tt=tt: oT4[:, tt, c, :], rstdR[tt - 2],
                                              l, P_GFFN))

                if hf == 0:
                    fill_b1(None)
                else:
                    for tt in (0, 1):
                        dq.extend(make_pieces(l, P_GPFFN, tt, lambda c, tt=tt: hT[:, c, tsl(tt)], rstdR[tt],
                                              (l + 1) if l + 1 < NL else None, P_GPRE))
                for cp0 in range(0, NCP, 2):
                    if hf == 0 and cp0 == 8:
                        steps.append((None, lambda _s: drain(len(dq))))
                    steps.append((ld_up(cp0), comp_up(cp0)))
                run_steps()
                pend.pop()()
                drain(len(dq))

                def ld_dn(oc):
                    return lambda: wload([(0, NCP, 128, wsrc(w_dn_d, l, NCP, oc * 128, 128))], three=True)

                def comp_dn(oc):
                    def f(s):
                        W = wview(s, 0, NCP, 128)
                        for i in range(2):
                            psi = next_ps(0, 4)
                            for cp in range(NCP):
                                MM(ps[psi][:, :], W[:, cp, :], actT[:, cp, tsl(i)], cp == 0, cp == NCP - 1)
                            if sspend:
                                sspend.pop()()
                            sq = osq[ok["n"] % 2]
                            ok["n"] += 1
                            COPY("act", hT[:, oc, tsl(2 * hf + i)], ps[psi][:, :])
                            ACT(sq, ps[psi][:, :], AF.Square)
                            sspend.append(lambda sq=sq, i=i, oc=oc: MM(ps[4 + i][:, :], ones, sq, oc == 0, oc == KC - 1))
                    return f

                for oc in range(KC):
                    steps.append((ld_dn(oc), comp_dn(oc)))

                def rstd_step(_s):
                    sspend.pop()()
                    rstd_ps(4, rstdR[0])
                    rstd_ps(5, rstdR[1])
                steps.append((None, rstd_step))
                run_steps(depth=2)
                if hf == 1 and l + 1 < NL:
                    steps.extend(early_proj_steps(l + 1))
                    run_steps()
                if hf == 1:
                    for tt in (2, 3):
                        resid_apply(l, P_GPFFN, tt, lambda c, tt=tt: hT[:, c, tsl(tt)], rstdR[tt - 2], [tmpA, tmpB])
                        if l + 1 < NL:
                            norm_tt(l + 1, P_GPRE, tt)

        for c in range(KC):
            final_events.append(DMA("sp", yT_d[c * 128:(c + 1) * 128, :], xT[:, c, :]))
        sc.emit(block, final_events)
    return nc


def _rel_bucket_np(dist):
    n = np.maximum(dist, 0)
    is_small = n < 16
    n_f = np.maximum(n, 16).astype(np.float32)
    v = (np.log(n_f / np.float32(16)) / np.float32(math.log(1024 / 16)) * np.float32(16))
    large = 16 + v.astype(np.int32)
    large = np.minimum(large, 31)
    return np.where(is_small, n, large)


def _host_prep(inp):
    f = np.float32
    prm = np.zeros((128, DEPTH * NPRM), f)
    for l in range(DEPTH):
        o = l * NPRM
        for name, off in (("g_pre_mix", P_GPRE), ("g_post_mix", P_GPOST), ("g_pre_ffn", P_GFFN),
                          ("g_post_ffn", P_GPFFN), ("g_mem", P_GMEM)):
            prm[:, o + off:o + off + 8] = np.asarray(inp[name][l], f).reshape(8, 128).T
        prm[:, o + P_BG:o + P_BG + 24] = np.asarray(inp["b_gate"][l], f).reshape(24, 128).T
        prm[:, o + P_CW:o + P_CW + 12] = np.asarray(inp["conv_mix_w"][l], f).reshape(12, 128).T
        prm[:, o + P_FW:o + P_FW + 132] = np.asarray(inp["ffn_conv_w"][l], f).reshape(132, 128).T
        prm[:, o + P_FB:o + P_FB + 44] = np.asarray(inp["ffn_conv_b"][l], f).reshape(44, 128).T
    rb = np.asarray(inp["rel_bias"], f)
    p = np.arange(128)[:, None]
    u = np.arange(EBW)[None, :]
    dist = u - 512 - p
    bucket = _rel_bucket_np(dist)
    tbl = np.empty((8, 128, EBW), f)
    for h in range(8):
        t = rb[bucket, h]
        tbl[h] = np.where(dist >= 0, t, f(NEGM))
    b31 = np.ascontiguousarray(np.broadcast_to(rb[31][None, :], (128, 8))).astype(f)
    cst = np.zeros((128, C_END), f)
    cst[:, C_ONES:C_ONES + 128] = 1.0
    cst[:, C_ID:C_ID + 128] = np.eye(128, dtype=f)
    for n in range(8):
        cst[n, C_E + n * 128:C_E + (n + 1) * 128] = 1.0
    for qb in range(4, 8):
        for n in range(qb):
            for m in range(qb):
                cst[n * 8 + m, C_AGG + (qb - 4) * 128 + n] = 1.0
    return prm, tbl, b31, cst


_W_NAMES = ("w_in", "w_conv_out", "w_attn_out", "w_mem_kv", "w_mem_out", "w_out", "w_up", "w_down")
_NC_CACHE = {}


def _in_maps(inputs, cores):
    prm, tbl, b31, cst = _host_prep(inputs)
    x = np.asarray(inputs["x"], np.float32)
    mem = np.asarray(inputs["mem"], np.float32)
    ws = {n: np.ascontiguousarray(np.asarray(inputs[n], np.float32)) for n in _W_NAMES}
    maps = []
    for b in cores:
        m = {"xT": np.ascontiguousarray(x[b].T), "memT": np.ascontiguousarray(mem[b].T),
             "prm": prm, "tbl": tbl, "b31": b31, "cst": cst}
        m.update(ws)
        maps.append(m)
    return maps


def kernel(**inputs):
    if "nc" not in _NC_CACHE:
        _NC_CACHE["nc"] = build_program(DEPTH)
    nc = _NC_CACHE["nc"]
    maps = _in_maps(inputs, list(range(8)))
    res = run_bass_kernel_spmd(nc, maps, core_ids=list(range(8)))
    out = np.stack([np.ascontiguousarray(r["yT"].T) for r in res.results], axis=0)
    return out.astype(np.float32)
```
